# Optimizing a Trainium2 kernel written in Bass

```python
import jax, jax.numpy as jnp
from jax import lax
import numpy as np

D_MODEL = 1024
BATCH = 4
SEQ = 8192
DEPTH = 2

CHUNK = 64
QBLOCK = 128
N_A = DEPTH // 2
N_B = DEPTH - N_A
FOX_HEADS = 16
FOX_HEAD_DIM = 64
FOX_WIDTH = FOX_HEADS * FOX_HEAD_DIM
MLA_HEADS = 16
MLA_NOPE = 128
MLA_ROPE = 64
MLA_V = 128
KV_LORA = 256
Q_LORA = 768
ROPE_THETA = 10000.0
D_FF = 2816
CONV_W = 3
EPS = 1e-6

kernel_name = 'hybrid_fox_mla_yoco_convffn'


def rmsnorm(x, g):
    xf = x.astype(jnp.float32)
    y = xf * lax.rsqrt(jnp.mean(xf * xf, axis=-1, keepdims=True) + EPS)
    return (y * g.astype(jnp.float32)).astype(x.dtype)


def rope_tables(seq, dtype):
    pos = jnp.arange(seq, dtype=jnp.float32)
    inv = ROPE_THETA ** (-jnp.arange(0, MLA_ROPE, 2, dtype=jnp.float32) / MLA_ROPE)
    ang = pos[:, None] * inv[None, :]
    return jnp.cos(ang).astype(dtype), jnp.sin(ang).astype(dtype)


def apply_rope(x, cos, sin):
    x1, x2 = jnp.split(x, 2, axis=-1)
    return jnp.concatenate([x1 * cos - x2 * sin, x1 * sin + x2 * cos], axis=-1)


def to_blocks(a):
    b, s = a.shape[:2]
    a = a.reshape((b, s // QBLOCK, QBLOCK) + a.shape[2:])
    return jnp.moveaxis(a, 1, 0)


def from_blocks(a):
    a = jnp.moveaxis(a, 0, 1)
    return a.reshape((a.shape[0], a.shape[1] * a.shape[2]) + a.shape[3:])


def fox_mixer(h, w_in, b_f, w_out):
    b, s_len, _ = h.shape
    proj = h @ w_in
    qkv = proj[..., :3 * FOX_WIDTH].reshape(b, s_len, 3, FOX_HEADS, FOX_HEAD_DIM)
    q, k, v = qkv[:, :, 0], qkv[:, :, 1], qkv[:, :, 2]
    logf = jax.nn.log_sigmoid((proj[..., 3 * FOX_WIDTH:] + b_f).astype(jnp.float32))
    cum = jnp.cumsum(logf, axis=1)
    cum_k = jnp.transpose(cum, (0, 2, 1))
    kpos = jnp.arange(s_len)
    scale = FOX_HEAD_DIM ** -0.5

    def block(args):
        i, qi, ci = args
        sc = jnp.einsum('bqhd,bkhd->bhqk', qi, k, preferred_element_type=jnp.float32) * scale
        sc = sc + (jnp.transpose(ci, (0, 2, 1))[..., :, None] - cum_k[:, :, None, :])
        qpos = i * QBLOCK + jnp.arange(QBLOCK)
        sc = jnp.where(kpos[None, :] <= qpos[:, None], sc, -jnp.inf)
        p = jax.nn.softmax(sc, axis=-1).astype(v.dtype)
        return jnp.einsum('bhqk,bkhd->bqhd', p, v)

    o = lax.map(block, (jnp.arange(s_len // QBLOCK), to_blocks(q), to_blocks(cum)))
    return from_blocks(o).reshape(b, s_len, FOX_WIDTH) @ w_out


def mla_shared_kv(x, kv_in_g, w_dkv, kv_norm_g, w_uk, w_uv, cos, sin):
    b, s_len, _ = x.shape
    ckv = rmsnorm(x, kv_in_g) @ w_dkv
    latent = rmsnorm(ckv[..., :KV_LORA], kv_norm_g)
    k_rope = apply_rope(ckv[..., KV_LORA:], cos, sin)
    k_nope = (latent @ w_uk).reshape(b, s_len, MLA_HEADS, MLA_NOPE)
    v = (latent @ w_uv).reshape(b, s_len, MLA_HEADS, MLA_V)
    return k_nope, k_rope, v


def mla_mixer(h, w_dq, q_norm_g, w_uq, w_out, k_nope, k_rope, v, cos, sin):
    b, s_len, _ = h.shape
    q = (rmsnorm(h @ w_dq, q_norm_g) @ w_uq).reshape(b, s_len, MLA_HEADS, MLA_NOPE + MLA_ROPE)
    q_nope = q[..., :MLA_NOPE]
    q_rope = apply_rope(q[..., MLA_NOPE:], cos[:, None, :], sin[:, None, :])
    kpos = jnp.arange(s_len)
    scale = (MLA_NOPE + MLA_ROPE) ** -0.5

    def block(args):
        i, qn, qr = args
        sc = (jnp.einsum('bqhd,bkhd->bhqk', qn, k_nope, preferred_element_type=jnp.float32)
              + jnp.einsum('bqhr,bkr->bhqk', qr, k_rope, preferred_element_type=jnp.float32)) * scale
        qpos = i * QBLOCK + jnp.arange(QBLOCK)
        visible = kpos[None, :] < (qpos[:, None] // CHUNK + 1) * CHUNK
        sc = jnp.where(visible, sc, -jnp.inf)
        p = jax.nn.softmax(sc, axis=-1).astype(v.dtype)
        return jnp.einsum('bhqk,bkhd->bqhd', p, v)

    o = lax.map(block, (jnp.arange(s_len // QBLOCK), to_blocks(q_nope), to_blocks(q_rope)))
    return from_blocks(o).reshape(b, s_len, MLA_HEADS * MLA_V) @ w_out


def conv_ffn(h, w_in, conv_w, conv_b, w_out):
    s_len = h.shape[1]
    u = h @ w_in
    up = jnp.pad(u, ((0, 0), (CONV_W - 1, 0), (0, 0)))
    u = conv_b + sum(up[:, j:j + s_len] * conv_w[j] for j in range(CONV_W))
    gate, val = jnp.split(u, 2, axis=-1)
    return (jax.nn.silu(gate) * val) @ w_out


def setup_inputs(seed: int = 0) -> dict:
    key = jax.random.key(seed)
    ks = jax.random.split(key, 24)
    f32 = jnp.float32
    res = (2.0 * DEPTH) ** -0.5

    def nrm(k, shape, scale):
        return jax.random.normal(k, shape, f32) * scale

    def gain(k, shape):
        return 1.0 + 0.02 * jax.random.normal(k, shape, f32)

    w_fox_in = jnp.concatenate([
        nrm(ks[2], (N_A, D_MODEL, 3 * FOX_WIDTH), D_MODEL ** -0.5),
        nrm(ks[3], (N_A, D_MODEL, FOX_HEADS), 0.1 * D_MODEL ** -0.5)], axis=-1)
    return {
        'x': jax.random.normal(ks[0], (BATCH, SEQ, D_MODEL), f32),
        'fox_norm': gain(ks[1], (N_A, D_MODEL)),
        'w_fox_in': w_fox_in,
        'b_fox_f': jax.random.uniform(ks[4], (N_A, FOX_HEADS), f32, 1.0, 6.0),
        'w_fox_out': nrm(ks[5], (N_A, FOX_WIDTH, D_MODEL), FOX_WIDTH ** -0.5 * res),
        'kv_in_norm': gain(ks[6], (D_MODEL,)),
        'w_dkv': nrm(ks[7], (D_MODEL, KV_LORA + MLA_ROPE), D_MODEL ** -0.5),
        'kv_norm': gain(ks[8], (KV_LORA,)),
        'w_uk': nrm(ks[9], (KV_LORA, MLA_HEADS * MLA_NOPE), KV_LORA ** -0.5),
        'w_uv': nrm(ks[10], (KV_LORA, MLA_HEADS * MLA_V), KV_LORA ** -0.5),
        'mla_norm': gain(ks[11], (N_B, D_MODEL)),
        'w_dq': nrm(ks[12], (N_B, D_MODEL, Q_LORA), D_MODEL ** -0.5),
        'q_norm': gain(ks[13], (N_B, Q_LORA)),
        'w_uq': nrm(ks[14], (N_B, Q_LORA, MLA_HEADS * (MLA_NOPE + MLA_ROPE)), Q_LORA ** -0.5),
        'w_mla_out': nrm(ks[15], (N_B, MLA_HEADS * MLA_V, D_MODEL), (MLA_HEADS * MLA_V) ** -0.5 * res),
        'ffn_norm': gain(ks[16], (DEPTH, D_MODEL)),
        'w_ffn_in': nrm(ks[17], (DEPTH, D_MODEL, 2 * D_FF), D_MODEL ** -0.5),
        'ffn_conv_w': nrm(ks[18], (DEPTH, CONV_W, 2 * D_FF), CONV_W ** -0.5),
        'ffn_conv_b': nrm(ks[19], (DEPTH, 2 * D_FF), 0.01),
        'w_ffn_out': nrm(ks[20], (DEPTH, D_FF, D_MODEL), D_FF ** -0.5 * res),
        'final_norm': gain(ks[21], (D_MODEL,)),
    }


def reference(x, fox_norm, w_fox_in, b_fox_f, w_fox_out, kv_in_norm, w_dkv, kv_norm, w_uk, w_uv,
              mla_norm, w_dq, q_norm, w_uq, w_mla_out, ffn_norm, w_ffn_in, ffn_conv_w, ffn_conv_b,
              w_ffn_out, final_norm):
    cos, sin = rope_tables(x.shape[1], x.dtype)
    k_nope = k_rope = v = None
    for layer in range(DEPTH):
        if layer < N_A:
            x = x + fox_mixer(rmsnorm(x, fox_norm[layer]), w_fox_in[layer], b_fox_f[layer],
                              w_fox_out[layer])
        else:
            if layer == N_A:
                k_nope, k_rope, v = mla_shared_kv(x, kv_in_norm, w_dkv, kv_norm, w_uk, w_uv, cos, sin)
            j = layer - N_A
            x = x + mla_mixer(rmsnorm(x, mla_norm[j]), w_dq[j], q_norm[j], w_uq[j], w_mla_out[j],
                              k_nope, k_rope, v, cos, sin)
        x = x + conv_ffn(rmsnorm(x, ffn_norm[layer]), w_ffn_in[layer], ffn_conv_w[layer],
                         ffn_conv_b[layer], w_ffn_out[layer])
    return rmsnorm(x, final_norm)
```

```python
import numpy as np
import ml_dtypes
from contextlib import ExitStack
import concourse.bass as bass
import concourse.mybir as mybir
from concourse.bass_utils import run_bass_kernel_spmd

F32 = mybir.dt.float32
BF16 = mybir.dt.bfloat16
AF = mybir.ActivationFunctionType
ALU = mybir.AluOpType

D = 1024
S = 8192
NB = 4
NT = S // 128
NG = S // 512
DFF = 2816
NFC = DFF // 128
EPS = 1e-6
NEG = -30000.0
OWN = S // 2
HALO = 128
NTOK = OWN + HALO


class Buf:
    __slots__ = ("name", "w", "r", "frozen")

    def __init__(self, name):
        self.name = name
        self.w = None
        self.r = {}
        self.frozen = False


class Rec:
    ENG = ("pe", "act", "dve", "pool", "sp")
    NLANES = {"sp": 8, "pool": 6, "act": 2}

    def __init__(self, nc, sems):
        self.nc = nc
        self.sems = sems
        self.cnt = {k: 0 for k in sems}
        self.seen = {e: {} for e in self.ENG}
        self.stream = {e: [] for e in self.ENG}
        self.lane_rr = {q: 0 for q in self.NLANES}
        self.log = None

    def _need(self, eng, tok):
        if tok is None:
            return
        key, val = tok
        if val <= 0:
            return
        if key == eng and eng == "pe":
            return
        if self.seen[eng].get(key, 0) >= val:
            return
        self.seen[eng][key] = val
        self.stream[eng].append(("wait", key, val))

    def _deps(self, eng, reads, writes, extra=()):
        for b in reads:
            self._need(eng, b.w)
        for b in writes:
            if b.w is not None and b.w[0] != eng:
                self._need(eng, b.w)
            for k, v in b.r.items():
                if k != eng:
                    self._need(eng, (k, v))
        for t in extra:
            self._need(eng, t)

    def _post(self, tok, reads, writes):
        k, v = tok
        for b in reads:
            if not b.frozen and b.r.get(k, 0) < v:
                b.r[k] = v
        for b in writes:
            b.w = tok
            b.r = {}

    def op(self, eng, fn, reads=(), writes=(), mark=True, extra=()):
        self._deps(eng, reads, writes, extra)
        if mark:
            self.cnt[eng] += 1
            tok = (eng, self.cnt[eng])
            self.stream[eng].append(("op", fn, eng, 1))
        else:
            tok = (eng, self.cnt[eng] + 1)
            self.stream[eng].append(("op", fn, None, 0))
        self._post(tok, reads, writes)
        return tok

    def dma(self, q, out, in_, reads=(), writes=(), extra=()):
        n = self.NLANES[q]
        lane = "%s_l%d" % (q, self.lane_rr[q] % n)
        self.lane_rr[q] += 1
        self._deps(q, reads, writes, extra)
        self._need(q, (lane, self.cnt[lane]))
        self.cnt[lane] += 16
        tok = (lane, self.cnt[lane])
        self.stream[q].append(("op", lambda e, o=out, i=in_: e.dma_start(out=o, in_=i), lane, 16))
        self._post(tok, reads, writes)
        return tok

    def barrier(self):
        for e in self.ENG:
            for k in self.cnt:
                self._need(e, (k, self.cnt[k]))

    def flush(self):
        nc = self.nc
        streams = self.stream
        self.stream = {e: [] for e in self.ENG}
        sems = self.sems
        if self.log is not None:
            self.log.append(streams)
            return

        def run(eng_obj, items):
            for it in items:
                if it[0] == "wait":
                    eng_obj.wait_ge(sems[it[1]], it[2])
                else:
                    ins = it[1](eng_obj)
                    if it[2] is not None:
                        ins.then_inc(sems[it[2]], it[3])

        with nc.Block() as block:
            if streams["pe"]:
                @block.tensor
                def _(e):
                    run(e, streams["pe"])
            if streams["act"]:
                @block.scalar
                def _(e):
                    run(e, streams["act"])
            if streams["dve"]:
                @block.vector
                def _(e):
                    run(e, streams["dve"])
            if streams["pool"]:
                @block.gpsimd
                def _(e):
                    run(e, streams["pool"])
            if streams["sp"]:
                @block.sync
                def _(e):
                    run(e, streams["sp"])


def sem_keys():
    keys = list(Rec.ENG)
    for q, n in Rec.NLANES.items():
        keys += ["%s_l%d" % (q, i) for i in range(n)]
    return keys


class Ctx:
    def __init__(self, nc, es):
        self.nc = nc
        self.es = es
        sems = {k: es.enter_context(nc.semaphore(k)) for k in sem_keys()}
        self.P = Rec(nc, sems)
        self.nbuf = 0
        self.pfx = ""

    def sb(self, es, name, shape, dt):
        return es.enter_context(self.nc.sbuf_tensor(self.pfx + name, shape, dt))

    def ps(self, es, name, shape, dt):
        return es.enter_context(self.nc.psum_tensor(self.pfx + name, shape, dt))

    def buf(self, name=None):
        self.nbuf += 1
        return Buf(name or ("b%d" % self.nbuf))

    def dram_in(self, name, shape, dt):
        return self.nc.dram_tensor(name, list(shape), dt, kind="ExternalInput").ap()

    def dram_out(self, name, shape, dt):
        return self.nc.dram_tensor(name, list(shape), dt, kind="ExternalOutput").ap()

    def dram_scr(self, name, shape, dt):
        return self.nc.dram_tensor(self.pfx + name, list(shape), dt).ap()


def load_consts(C, es):
    P = C.P
    C.cb_d = C.dram_in("cst_bf", [128, 512], BF16)
    C.cf_d = C.dram_in("cst_f", [128, 384], F32)
    C.cb = C.sb(es, "cst_bf_sb", [128, 512], BF16)
    C.cf = C.sb(es, "cst_f_sb", [128, 384], F32)
    C.cbB = C.buf("cb")
    C.cfB = C.buf("cf")
    P.dma("sp", C.cb[:, :], C.cb_d, writes=[C.cbB])
    P.dma("sp", C.cf[:, :], C.cf_d, writes=[C.cfB])
    C.cbB.frozen = True
    C.cfB.frozen = True
    C.ident = C.cb[:, 0:128]
    C.mask_fox = C.cb[:, 128:256]
    C.mask_mla = C.cb[:, 256:384]
    C.ones_bf = C.cb[:, 384:512]
    C.identf = C.cf[:, 0:128]
    C.utri = C.cf[:, 128:256]
    C.onesf = C.cf[:, 256:384]


def host_consts():
    cb = np.zeros((128, 512), np.float32)
    cb[:, 0:128] = np.eye(128)
    p = np.arange(128)[:, None]
    c = np.arange(128)[None, :]
    cb[:, 128:256] = np.where(p <= c, 0.0, NEG)
    cb[:, 256:384] = np.where((p >= 64) & (c < 64), NEG, 0.0)
    cb[:, 384:512] = 1.0
    cf = np.zeros((128, 384), np.float32)
    cf[:, 0:128] = np.eye(128)
    cf[:, 128:256] = (p <= c).astype(np.float32)
    cf[:, 256:384] = 1.0
    return {"cst_bf": cb.astype(ml_dtypes.bfloat16), "cst_f": cf}


def load_weight_bf16(C, es_stage, w_d, dst, dstB, nchunk, ncol, gcol=None, gB=None, colblk=None, tag="w"):
    P = C.P
    colblk = colblk or ncol
    NSTG = 3
    stg = [C.sb(es_stage, "%s_stg%d" % (tag, i), [128, colblk], F32) for i in range(NSTG)]
    stgB = [C.buf() for _ in range(NSTG)]
    wv = w_d.rearrange("(c p) f -> p c f", p=128)
    k = 0
    for c in range(nchunk):
        for c0 in range(0, ncol, colblk):
            i = k % NSTG
            k += 1
            P.dma("sp", stg[i][:, :], wv[:, c, c0:c0 + colblk], writes=[stgB[i]])
            eng = ("dve", "act", "pool", "dve", "act")[k % 5]
            if eng == "act":
                if gcol is not None:
                    P.op("act", lambda e, i=i, c=c, c0=c0: e.activation(
                        out=dst[:, c, c0:c0 + colblk], in_=stg[i][:, :], func=AF.Copy, scale=gcol[:, c:c + 1]),
                        reads=[stgB[i]] + ([gB] if gB else []), writes=[dstB])
                else:
                    P.op("act", lambda e, i=i, c=c, c0=c0: e.activation(out=dst[:, c, c0:c0 + colblk], in_=stg[i][:, :], func=AF.Copy),
                         reads=[stgB[i]], writes=[dstB])
            elif gcol is not None:
                P.op(eng, lambda e, i=i, c=c, c0=c0: e.tensor_scalar(
                    out=dst[:, c, c0:c0 + colblk], in0=stg[i][:, :], scalar1=gcol[:, c:c + 1], scalar2=None, op0=ALU.mult),
                    reads=[stgB[i]] + ([gB] if gB else []), writes=[dstB])
            else:
                P.op(eng, lambda e, i=i, c=c, c0=c0: e.tensor_copy(out=dst[:, c, c0:c0 + colblk], in_=stg[i][:, :]),
                     reads=[stgB[i]], writes=[dstB])


class NormT:
    def __init__(self, C, es, tag, width=D):
        self.C = C
        self.width = width
        self.nch = width // 128
        self.junk = C.sb(es, tag + "_junk", [128, width], F32)
        self.junkB = C.buf()
        self.st = [C.sb(es, tag + "_st%d" % i, [128, 4], F32) for i in range(2)]
        self.stB = [C.buf() for _ in range(2)]
        self.xn = [C.sb(es, tag + "_xn%d" % i, [128, width], BF16) for i in range(2)]
        self.xnB = [C.buf() for _ in range(2)]
        self.k = 0

    def rstd(self, src_ap, srcB, parts=None):
        C, P = self.C, self.C.P
        i = self.k % 2
        self.k += 1
        st, stB = self.st[i], self.stB[i]
        parts = parts or [(src_ap, self.width)]
        off = 0
        for j, (ap, w) in enumerate(parts):
            P.op("act", lambda e, ap=ap, w=w, off=off, j=j: e.activation(
                out=self.junk[:, off:off + w], in_=ap, func=AF.Square, accum_out=st[:, 2 + j:3 + j]),
                reads=[srcB], writes=[self.junkB, stB])
            off += w
        if len(parts) == 2:
            P.op("dve", lambda e: e.tensor_tensor(out=st[:, 2:3], in0=st[:, 2:3], in1=st[:, 3:4], op=ALU.add),
                 reads=[stB], writes=[stB])
        P.op("act", lambda e: e.activation(out=st[:, 0:1], in_=st[:, 2:3], func=AF.Sqrt, scale=1.0 / self.width, bias=EPS),
             reads=[stB], writes=[stB])
        P.op("dve", lambda e: e.reciprocal(out=st[:, 1:2], in_=st[:, 0:1]), reads=[stB], writes=[stB])
        return i

    def normalize(self, i, parts, srcB):
        P = self.C.P
        st, stB = self.st[i], self.stB[i]
        off = 0
        for (ap, w) in parts:
            P.op("act", lambda e, ap=ap, w=w, off=off: e.activation(
                out=self.xn[i][:, off:off + w], in_=ap, func=AF.Copy, scale=st[:, 1:2]),
                reads=[srcB, stB], writes=[self.xnB[i]])
            off += w
        return self.xn[i], self.xnB[i]

    def transpose_to(self, i, pT, pTB, dst_fn, dstB):
        C, P = self.C, self.C.P
        for c in range(self.nch):
            P.op("pe", lambda e, c=c: e.transpose(out=pT[:, c, :], in_=self.xn[i][:, c * 128:(c + 1) * 128], identity=C.ident),
                 reads=[self.xnB[i], C.cbB], writes=[pTB], mark=(c == self.nch - 1))
        P.op("dve", lambda e: e.tensor_copy(out=dst_fn(), in_=pT[:, 0:self.nch, :]), reads=[pTB], writes=[dstB])


def attention_phase(C, es, n_heads, dv, kparts, load_head, out_d, scale, mask_ap, bias_fn, den_mode, after_head=None):
    P = C.P
    NPS = 5 if den_mode == "row64" else 4
    ps = [C.ps(es, "att_ps%d" % i, [128, 512], F32) for i in range(NPS)]
    psB = [C.buf() for _ in range(NPS)]
    po = [C.ps(es, "att_po%d" % i, [128, 512], F32) for i in range(2)]
    poB = [C.buf() for _ in range(2)]
    if den_mode == "sep":
        pl = [C.ps(es, "att_pl%d" % i, [128, 512], F32) for i in range(2)]
        plB = [C.buf() for _ in range(2)]
        acc = [C.sb(es, "att_acc%d" % i, [128, 512], F32) for i in range(2)]
        accB = [C.buf() for _ in range(2)]
    else:
        pb = C.ps(es, "att_pb", [128, 512], F32)
        pbB = C.buf()
    pt = [C.sb(es, "att_pt%d" % i, [128, 512], BF16) for i in range(NPS)]
    ptB = [C.buf() for _ in range(NPS)]
    rl = C.sb(es, "att_rl", [128, 512], F32)
    rlB = C.buf()
    bc = C.sb(es, "att_bc", [128, 512], F32)
    bcB = C.buf()
    on = [C.sb(es, "att_on%d" % i, [128, 512], BF16) for i in range(2)]
    onB = [C.buf() for _ in range(2)]
    drow = dv if den_mode == "row64" else 0
    P.op("pool", lambda e: e.memset(rl[:, :], 0.0), writes=[rlB])

    gcount = [0]
    ucount = [0]
    outB = {}
    load_head(0)
    for h in range(n_heads):
        if h + 1 < n_heads:
            load_head(h + 1)
        hb = h % 2
        parts, partB, v_fn, vB = kparts(h)
        units = [(g, kt) for g in range(NG) for kt in range(4 * (g + 1))]
        n = len(units)
        LOOK = NPS - 1
        state = {}

        def emit_qk(u, idx):
            g, kt = u
            i = kt - 4 * g
            c0 = 128 * i if i >= 0 else 0
            b = idx % NPS
            np_ = len(parts)
            for pi, (kf, qf) in enumerate(parts):
                last = (pi == np_ - 1) and (i < 0)
                P.op("pe", lambda e, kf=kf, qf=qf, pi=pi, last=last, b=b, c0=c0, g=g, kt=kt: e.matmul(
                    ps[b][:, c0:512], lhsT=kf(kt), rhs=qf(g * 512 + c0, (g + 1) * 512), start=(pi == 0), stop=last,
                    skip_group_check=True),
                    reads=partB, writes=[psB[b]], mark=last)
            if i >= 0:
                P.op("pe", lambda e, b=b, c0=c0: e.matmul(ps[b][:, c0:c0 + 128], lhsT=C.ident, rhs=mask_ap,
                                                          start=False, stop=True, skip_group_check=True),
                     reads=[C.cbB], writes=[psB[b]], mark=True)

        def emit_rest(u, idx):
            g, kt = u
            i = kt - 4 * g
            c0 = 128 * i if i >= 0 else 0
            b = idx % NPS
            nkt = 4 * (g + 1)
            gi = state.setdefault(g, None)
            if kt == 0:
                state[g] = gcount[0] % 2
                gcount[0] += 1
            ob = state[g]
            bias = bias_fn(h, kt)
            if bias is not None:
                bias_ap, biasB = bias
                P.op("act", lambda e, b=b, c0=c0, bias_ap=bias_ap: e.activation(
                    out=pt[b][:, c0:512], in_=ps[b][:, c0:512], func=AF.Exp, bias=bias_ap, scale=scale),
                    reads=[psB[b], biasB], writes=[ptB[b]])
            else:
                P.op("act", lambda e, b=b, c0=c0: e.activation(
                    out=pt[b][:, c0:512], in_=ps[b][:, c0:512], func=AF.Exp, scale=scale),
                    reads=[psB[b]], writes=[ptB[b]])
            mrows = dv + 1 if den_mode == "row64" else dv
            P.op("pe", lambda e, b=b, c0=c0, kt=kt, ob=ob, nkt=nkt, mrows=mrows, vf=v_fn: e.matmul(
                po[ob][0:mrows, c0:512], lhsT=vf(kt), rhs=pt[b][:, c0:512], start=(kt == 0), stop=(kt == nkt - 1),
                skip_group_check=True),
                reads=[ptB[b], vB], writes=[poB[ob]], mark=True)
            if den_mode == "sep":
                if kt == 0:
                    P.op("dve", lambda e, b=b, ob=ob: e.tensor_copy(out=acc[ob][:, :], in_=pt[b][:, :]),
                         reads=[ptB[b]], writes=[accB[ob]])
                elif kt % 3 != 1:
                    P.op("dve", lambda e, b=b, ob=ob, c0=c0: e.tensor_tensor(
                        out=acc[ob][:, c0:512], in0=acc[ob][:, c0:512], in1=pt[b][:, c0:512], op=ALU.add),
                        reads=[ptB[b], accB[ob]], writes=[accB[ob]])
                else:
                    P.op("pe", lambda e, b=b, c0=c0, kt=kt, ob=ob: e.matmul(
                        pl[ob][:, c0:512], lhsT=C.ones_bf[:, 0:128], rhs=pt[b][:, c0:512], start=(kt == 1), stop=False,
                        skip_group_check=True),
                        reads=[ptB[b], C.cbB], writes=[plB[ob]], mark=True)
            if kt == nkt - 1:
                oi = ob
                if den_mode == "sep":
                    P.op("pe", lambda e, ob=ob: e.matmul(pl[ob][:, :], lhsT=C.onesf[:, 0:128], rhs=acc[ob][:, :], start=False, stop=True,
                                                         skip_group_check=True),
                         reads=[accB[ob], C.cfB], writes=[plB[ob]], mark=True)
                    P.op("dve", lambda e, ob=ob: e.reciprocal(out=bc[:, :], in_=pl[ob][:, :]), reads=[plB[ob]], writes=[bcB])
                else:
                    P.op("dve", lambda e, ob=ob: e.reciprocal(out=rl[drow:drow + 1, :], in_=po[ob][drow:drow + 1, :]),
                         reads=[poB[ob]], writes=[rlB])
                    P.op("pe", lambda e: e.matmul(pb[:, :], lhsT=C.onesf[:, 0:128], rhs=rl[:, :],
                                                  start=True, stop=True, skip_group_check=True),
                         reads=[rlB, C.cfB], writes=[pbB], mark=True)
                    P.op("act", lambda e: e.activation(out=bc[0:dv, :], in_=pb[0:dv, :], func=AF.Copy), reads=[pbB], writes=[bcB])
                P.op("dve", lambda e, ob=ob, oi=oi: e.tensor_tensor(out=on[oi][0:dv, :], in0=po[ob][0:dv, :], in1=bc[0:dv, :], op=ALU.mult),
                     reads=[poB[ob], bcB], writes=[onB[oi]])
                ob_ = C.buf()
                outB.setdefault(h, []).append(ob_)
                P.dma("sp", out_d[h * dv:(h + 1) * dv, g * 512:(g + 1) * 512], on[oi][0:dv, :], reads=[onB[oi]], writes=[ob_])

        for i in range(n + LOOK):
            if i < n:
                emit_qk(units[i], ucount[0] + i)
            j = i - LOOK
            if j >= 0:
                emit_rest(units[j], ucount[0] + j)
        ucount[0] += n
        if after_head is not None:
            after_head(h, outB)


def build_A():
    nc = bass.Bass("TRN2", target_bir_lowering=False)
    with ExitStack() as es0:
        C = Ctx(nc, es0)
        io = dict(x=C.dram_in("x", [S, D], F32), wq=C.dram_in("wq", [D, 512], F32), wk=C.dram_in("wk", [D, 512], F32),
                  wv=C.dram_in("wv", [D, 512], F32), wf=C.dram_in("wf", [D, 8], F32), bf=C.dram_in("bf", [128, 8], F32),
                  g=C.dram_in("gcol", [128, 8], F32), out=C.dram_out("onT", [512, S], BF16))
        load_consts(C, es0)
        emit_A(C, io)
    return nc


def emit_A(C, io):
    P = C.P
    C.pfx = "A_"
    with ExitStack() as es0:
        x_d, wq_d, wk_d, wv_d, wf_d, bf_d, g_d, out_d = (io[k_] for k_ in ("x", "wq", "wk", "wv", "wf", "bf", "g", "out"))
        Qs = C.dram_scr("Qs", [8, 67, S], BF16)
        Ks = C.dram_scr("Ks", [8, 64, S], BF16)
        Vs = C.dram_scr("Vs", [8, 128, NT, 65], BF16)
        QsB = [[C.buf() for _ in range(NG)] for _ in range(8)]
        KsB = [[C.buf() for _ in range(NG)] for _ in range(8)]
        VsB = [[C.buf() for _ in range(NG)] for _ in range(8)]
        cK = C.sb(es0, "cK", [128, NT, 8], F32)
        cKB = C.buf()

        with ExitStack() as es:
            gcol = C.sb(es, "gcol_sb", [128, 8], F32)
            gB = C.buf()
            P.dma("sp", gcol[:, :], g_d, writes=[gB])
            bfs = C.sb(es, "bf_sb", [128, 8], F32)
            bfB = C.buf()
            P.dma("sp", bfs[:, :], bf_d, writes=[bfB])
            wq = C.sb(es, "wq_sb", [128, 8, 576], BF16)
            wk = C.sb(es, "wk_sb", [128, 8, 576], BF16)
            wv = C.sb(es, "wv_sb", [128, 8, 512], BF16)
            wf = C.sb(es, "wf_sb", [128, 8, 8], BF16)
            wqB, wkB, wvB, wfB = C.buf(), C.buf(), C.buf(), C.buf()
            if True:
                es_st = es
                P.op("pool", lambda e: e.memset(wq[:, :, 512:576], 0.0), writes=[wqB])
                P.op("pool", lambda e: e.memset(wk[:, :, 512:576], 0.0), writes=[wkB])
                load_weight_bf16(C, es_st, wq_d, wq, wqB, 8, 512, gcol, gB, tag="wq")
                load_weight_bf16(C, es_st, wk_d, wk, wkB, 8, 512, gcol, gB, tag="wk")
                load_weight_bf16(C, es_st, wv_d, wv, wvB, 8, 512, gcol, gB, tag="wv")
                load_weight_bf16(C, es_st, wf_d, wf, wfB, 8, 8, gcol, gB, tag="wf")
                for b_ in (wqB, wkB, wvB, wfB):
                    b_.frozen = True
                nrm = NormT(C, es, "nA")
                xt = [C.sb(es, "xtA%d" % i, [128, D], F32) for i in range(3)]
                xtB = [C.buf() for _ in range(3)]
                hT = [C.sb(es, "hTA%d" % i, [128, 8, 512], BF16) for i in range(2)]
                hTB = [C.buf() for _ in range(2)]
                Qst = [C.sb(es, "Qst%d" % i, [64, 8, 512], BF16) for i in range(2)]
                Kst = [C.sb(es, "Kst%d" % i, [64, 8, 512], BF16) for i in range(2)]
                Vst = [C.sb(es, "Vst%d" % i, [128, 8, 4, 65], BF16) for i in range(2)]
                QstB = [C.buf() for _ in range(2)]
                KstB = [C.buf() for _ in range(2)]
                VstB = [C.buf() for _ in range(2)]
                lall = C.sb(es, "lall", [128, NT, 8], F32)
                lallB = C.buf()
                for i in range(2):
                    P.op("pool", lambda e, i=i: e.memset(Vst[i][:, :, :, 64:65], 1.0), writes=[VstB[i]])
                pT = C.ps(es, "pTA", [128, 8, 128], BF16)
                pTB = C.buf()
                pq = [C.ps(es, "pqA%d" % i, [128, 512], F32) for i in range(2)]
                pqB = [C.buf() for _ in range(2)]
                pv = [C.ps(es, "pvA%d" % i, [128, 512], F32) for i in range(2)]
                pvB = [C.buf() for _ in range(2)]
                pz = C.ps(es, "pzA", [128, 8], F32)
                pzB = C.buf()
                xv = x_d.rearrange("(n p) d -> n p d", p=128)
                nld = [0]

                def load_x(tile_idx):
                    i = tile_idx % 3
                    P.dma("sp", xt[i][:, :], xv[tile_idx], writes=[xtB[i]])

                load_x(0)
                load_x(1)
                qk = 0
                for tg in range(NG):
                    hb = tg % 2
                    for s in range(4):
                        ti = tg * 4 + s
                        if ti + 2 < NT:
                            load_x(ti + 2)
                        xi = ti % 3
                        si = nrm.rstd(xt[xi][:, :], xtB[xi])
                        nrm.normalize(si, [(xt[xi][:, :], D)], xtB[xi])
                        nrm.transpose_to(si, pT, pTB, lambda hb=hb, s=s: hT[hb][:, :, s * 128:(s + 1) * 128], hTB[hb])
                    for which, (w_sb, wB, st_, stB_) in enumerate(((wq, wqB, Qst, QstB), (wk, wkB, Kst, KstB))):
                        for h in range(8):
                            pi = qk % 2
                            qk += 1
                            for c in range(8):
                                P.op("pe", lambda e, pi=pi, c=c, h=h, w_sb=w_sb, hb=hb: e.matmul(
                                    pq[pi][:, :], lhsT=w_sb[:, c, h * 64:h * 64 + 128], rhs=hT[hb][:, c, :],
                                    start=(c == 0), stop=(c == 7)),
                                    reads=[wB, hTB[hb]], writes=[pqB[pi]], mark=(c == 7))
                            eng = "act" if (h % 2 == 1) else "dve"
                            if eng == "act":
                                P.op("act", lambda e, pi=pi, h=h, st_=st_, hb=hb: e.activation(
                                    out=st_[hb][0:64, h, :], in_=pq[pi][0:64, :], func=AF.Copy),
                                    reads=[pqB[pi]], writes=[stB_[hb]])
                            else:
                                P.op("dve", lambda e, pi=pi, h=h, st_=st_, hb=hb: e.tensor_copy(
                                    out=st_[hb][0:64, h, :], in_=pq[pi][0:64, :]),
                                    reads=[pqB[pi]], writes=[stB_[hb]])
                    P.dma("pool", Qs[:, 0:64, tg * 512:(tg + 1) * 512].rearrange("h p t -> p h t"), Qst[hb][:, :, :],
                          reads=[QstB[hb]], writes=[QsB[h_][tg] for h_ in range(8)])
                    P.dma("pool", Ks[:, :, tg * 512:(tg + 1) * 512].rearrange("h p t -> p h t"), Kst[hb][:, :, :],
                          reads=[KstB[hb]], writes=[KsB[h_][tg] for h_ in range(8)])
                    for s in range(4):
                        pi = s % 2
                        for c in range(8):
                            P.op("pe", lambda e, pi=pi, c=c, s=s, hb=hb: e.matmul(
                                pv[pi][:, :], lhsT=hT[hb][:, c, s * 128:(s + 1) * 128], rhs=wv[:, c, :],
                                start=(c == 0), stop=(c == 7)),
                                reads=[wvB, hTB[hb]], writes=[pvB[pi]], mark=(c == 7))
                        P.op("act", lambda e, pi=pi, s=s, hb=hb: e.activation(
                            out=Vst[hb][:, :, s, 0:64], in_=pv[pi][:, :].rearrange("p (h d) -> p h d", d=64), func=AF.Copy),
                            reads=[pvB[pi]], writes=[VstB[hb]])
                        for c in range(8):
                            P.op("pe", lambda e, c=c, s=s, hb=hb: e.matmul(
                                pz[:, :], lhsT=hT[hb][:, c, s * 128:(s + 1) * 128], rhs=wf[:, c, :],
                                start=(c == 0), stop=(c == 7)),
                                reads=[wfB, hTB[hb]], writes=[pzB], mark=(c == 7))
                        P.op("dve", lambda e, s=s, tg=tg: e.tensor_tensor(
                            out=lall[:, tg * 4 + s, :], in0=pz[:, :], in1=bfs[:, :], op=ALU.add),
                            reads=[pzB, bfB], writes=[lallB])
                    for h in range(8):
                        P.dma("pool", Vs[h, :, tg * 4:(tg + 1) * 4, :], Vst[hb][:, h, :, :],
                              reads=[VstB[hb]], writes=[VsB[h][tg]])

                lf = lall[:, :, :].rearrange("p j h -> p (j h)")
                P.op("act", lambda e: e.activation(out=lf, in_=lf, func=AF.Exp, scale=-1.0), reads=[lallB], writes=[lallB])
                P.op("act", lambda e: e.activation(out=lf, in_=lf, func=AF.Ln, bias=1.0), reads=[lallB], writes=[lallB])
                pw = pq[0]
                ptot = pq[1]
                P.op("pe", lambda e: e.matmul(pw[:, :], lhsT=C.utri, rhs=lf, start=True, stop=True),
                     reads=[lallB, C.cfB], writes=[pqB[0]])
                P.op("pe", lambda e: e.matmul(ptot[:, :], lhsT=C.onesf, rhs=lf, start=True, stop=True),
                     reads=[lallB, C.cfB], writes=[pqB[1]])
                sc = [C.sb(es, "scan%d" % i, [128, NT, 8], F32) for i in range(2)]
                scB = [C.buf() for _ in range(2)]
                P.op("act", lambda e: e.activation(out=sc[0][:, :, :].rearrange("p j h -> p (j h)"), in_=ptot[:, :], func=AF.Copy),
                     reads=[pqB[1]], writes=[scB[0]])
                cur = 0
                dd = 1
                while dd < NT:
                    nxt = 1 - cur
                    P.op("dve", lambda e, cur=cur, nxt=nxt, dd=dd: e.tensor_copy(out=sc[nxt][:, 0:dd, :], in_=sc[cur][:, 0:dd, :]),
                         reads=[scB[cur]], writes=[scB[nxt]])
                    P.op("dve", lambda e, cur=cur, nxt=nxt, dd=dd: e.tensor_tensor(
                        out=sc[nxt][:, dd:NT, :], in0=sc[cur][:, dd:NT, :], in1=sc[cur][:, 0:NT - dd, :], op=ALU.add),
                        reads=[scB[cur]], writes=[scB[nxt]])
                    cur = nxt
                    dd *= 2
                P.op("dve", lambda e: e.tensor_copy(out=cK[:, 0:1, :], in_=pw[:, 0:8].rearrange("p (j h) -> p j h", h=8)),
                     reads=[pqB[0]], writes=[cKB])
                P.op("dve", lambda e, cur=cur: e.tensor_tensor(
                    out=cK[:, 1:NT, :], in0=pw[:, 8:NT * 8].rearrange("p (j h) -> p j h", h=8), in1=sc[cur][:, 0:NT - 1, :], op=ALU.add),
                    reads=[pqB[0], scB[cur]], writes=[cKB])
                c8 = C.sb(es, "c8", [128, NT * 8], F32)
                t32 = C.sb(es, "t32", [128, NT * 8], F32)
                c8B, t32B = C.buf(), C.buf()
                pcs = [C.sb(es, "pcs%d" % i, [128, NT, 8], BF16) for i in range(3)]
                pcsB = [C.buf() for _ in range(3)]
                P.op("act", lambda e: e.activation(out=c8[:, :], in_=cK[:, :, :].rearrange("p j h -> p (j h)"), func=AF.Copy, scale=-8.0),
                     reads=[cKB], writes=[c8B])
                for i in range(3):
                    P.op("dve", lambda e, i=i: e.tensor_copy(out=pcs[i][:, :, :].rearrange("p j h -> p (j h)"), in_=c8[:, :]),
                         reads=[c8B], writes=[pcsB[i]])
                    if i < 2:
                        P.op("dve", lambda e, i=i: e.tensor_copy(out=t32[:, :], in_=pcs[i][:, :, :].rearrange("p j h -> p (j h)")),
                             reads=[pcsB[i]], writes=[t32B])
                        P.op("dve", lambda e: e.tensor_tensor(out=c8[:, :], in0=c8[:, :], in1=t32[:, :], op=ALU.subtract),
                             reads=[c8B, t32B], writes=[c8B])
                pTc = C.ps(es, "pTc", [128, 4, 512], BF16)
                pTcB = C.buf()
                cst_ = [C.sb(es, "cstg%d" % i, [8, 3, 512], BF16) for i in range(2)]
                cstB = [C.buf() for _ in range(2)]
                for r in range(NG):
                    for i in range(3):
                        for s_ in range(4):
                            P.op("pe", lambda e, r=r, i=i, s_=s_: e.transpose(
                                out=pTc[0:8, i, s_ * 128:(s_ + 1) * 128], in_=pcs[i][:, r * 4 + s_, :], identity=C.ident),
                                reads=[pcsB[i], C.cbB], writes=[pTcB], mark=(i == 2 and s_ == 3))
                    P.op("dve", lambda e, r=r: e.tensor_copy(out=cst_[r % 2][:, :, :], in_=pTc[0:8, 0:3, :]),
                         reads=[pTcB], writes=[cstB[r % 2]])
                    P.dma("pool", Qs[:, 64:67, r * 512:(r + 1) * 512], cst_[r % 2][:, :, :], reads=[cstB[r % 2]],
                          writes=[QsB[h_][r] for h_ in range(8)])
                cKB.frozen = True
                P.barrier()
                P.flush()

        with ExitStack() as es:
            Qh = [C.sb(es, "Qh%d" % i, [67, S], BF16) for i in range(2)]
            Kh = [C.sb(es, "Kh%d" % i, [67, S], BF16) for i in range(2)]
            Vh = [C.sb(es, "Vh%d" % i, [128, NT, 65], BF16) for i in range(2)]
            QhB = [C.buf() for _ in range(2)]
            KhB = [C.buf() for _ in range(2)]
            VhB = [C.buf() for _ in range(2)]
            for i in range(2):
                P.op("pool", lambda e, i=i: e.memset(Kh[i][64:67, :], 1.0), writes=[KhB[i]])

            def load_head(h):
                hb = h % 2
                P.dma("sp", Qh[hb][:, :], Qs[h], reads=QsB[h], writes=[QhB[hb]])
                P.dma("sp", Kh[hb][0:64, :], Ks[h], reads=KsB[h], writes=[KhB[hb]])
                P.dma("sp", Vh[hb][:, :, :], Vs[h], reads=VsB[h], writes=[VhB[hb]])

            def kparts(h):
                hb = h % 2
                parts = [(lambda kt, hb=hb: Kh[hb][:, kt * 128:(kt + 1) * 128],
                          lambda a, b, hb=hb: Qh[hb][:, a:b])]
                return parts, [KhB[hb], QhB[hb]], (lambda kt, hb=hb: Vh[hb][:, kt, :]), VhB[hb]

            def bias_fn(h, kt):
                return cK[:, kt, h:h + 1], cKB

            attention_phase(C, es, 8, 64, kparts, load_head, out_d, 0.125, C.mask_fox, bias_fn, "row64", after_head=io.get("after_head"))
            P.barrier()
            P.flush()


def col128(v):
    v = np.asarray(v, np.float32)
    return np.ascontiguousarray(v.reshape(-1, 128).T)


def inputs_A(inp, b, hh, consts):
    w = np.asarray(inp["w_fox_in"][0])
    hs = slice(hh * 512, (hh + 1) * 512)
    d = {
        "x": np.ascontiguousarray(np.asarray(inp["x"][b], np.float32)),
        "wq": np.ascontiguousarray(w[:, 0:1024][:, hs]),
        "wk": np.ascontiguousarray(w[:, 1024:2048][:, hs]),
        "wv": np.ascontiguousarray(w[:, 2048:3072][:, hs]),
        "wf": np.ascontiguousarray(w[:, 3072 + hh * 8:3072 + (hh + 1) * 8]),
        "bf": np.ascontiguousarray(np.broadcast_to(np.asarray(inp["b_fox_f"][0], np.float32)[hh * 8:(hh + 1) * 8][None, :], (128, 8))),
        "gcol": col128(inp["fox_norm"][0]),
    }
    d.update(consts)
    return d


def build_B(RC, final):
    nc = bass.Bass("TRN2", target_bir_lowering=False)
    with ExitStack() as es0:
        C = Ctx(nc, es0)
        onT_d = C.dram_in("onT", [RC * 128, NTOK], BF16)
        x_d = C.dram_in("x", [NTOK, D], F32)
        onv = onT_d.rearrange("(c p) t -> p c t", p=128)
        io = dict(wo=C.dram_in("wo", [RC * 128, D], F32), g=C.dram_in("gcol", [128, 8], F32),
                  w_in=C.dram_in("w_in", [D, 2 * DFF], F32), cw=C.dram_in("cw", [128, 2 * NFC, 3], F32),
                  cb=C.dram_in("cb", [128, 2 * NFC], F32), w_out=C.dram_in("w_out", [DFF, D], F32),
                  out=C.dram_out("xo", [OWN, D], F32),
                  on_cands=lambda t0, T: [(onv[:, :, t0:t0 + T], None)],
                  x_src=lambda t0, T: (x_d[t0:t0 + T, :].rearrange("(s p) d -> p s d", p=128), None))
        if final:
            io["gfin"] = C.dram_in("gfin", [128, D], F32)
        load_consts(C, es0)
        emit_B(C, RC, final, "L%d_" % int(final), io)
    return nc


def emit_B(C, RC, final, pfx, io):
    P = C.P
    C.pfx = pfx
    with ExitStack() as es0:
        wo_d, g_d, win_d, cw_d, cb_d, wout_d, out_d = (io[k_] for k_ in ("wo", "g", "w_in", "cw", "cb", "w_out", "out"))
        if final:
            gf_d = io["gfin"]
        xm_d = C.dram_scr("xm", [NTOK, D], F32)
        rk = io.get("rk")
        rkB = io.get("rkB")
        groups1 = [(0, 128)] + [(128 + i * 512, 512) for i in range(OWN // 512)]
        xmB = {}

        with ExitStack() as es:
            wo = C.sb(es, "wo_sb", [128, RC, D], BF16)
            woB = C.buf()
            load_weight_bf16(C, es, wo_d, wo, woB, RC, D, tag="wo")
            woB.frozen = True
            onT = [C.sb(es, "onT%d" % i, [128, RC, 512], BF16) for i in range(2)]
            onTB = [C.buf() for _ in range(2)]
            xs = [C.sb(es, "xs%d" % i, [128, 4, D], F32) for i in range(2)]
            xsB = [C.buf() for _ in range(2)]
            xm = [C.sb(es, "xm%d" % i, [128, 4, D], F32) for i in range(2)]
            xmsB = [C.buf() for _ in range(2)]
            py = [C.ps(es, "pyB%d" % i, [128, 512], F32) for i in range(4)]
            pyB = [C.buf() for _ in range(4)]
            ncand = max(len(io["on_cands"](t0_, T_)) for (t0_, T_) in groups1)
            blend = any(m_ is not None for (t0_, T_) in groups1 for (_, m_) in io["on_cands"](t0_, T_))
            if blend:
                cnd = [C.sb(es, "cnd%d" % i, [128, RC, 512], BF16) for i in range(ncand)]
                cndB = [C.buf() for _ in range(ncand)]

            def load_grp(gi):
                t0, T = groups1[gi]
                b = gi % 2
                cands = io["on_cands"](t0, T)
                if not blend:
                    P.dma("sp", onT[b][:, :, 0:T], cands[0][0], writes=[onTB[b]])
                else:
                    for ci, (ap, m_) in enumerate(cands):
                        P.dma("sp", cnd[ci][:, :, 0:T], ap, writes=[cndB[ci]])
                    for ci, (ap, m_) in enumerate(cands):
                        if ci == 0:
                            P.op("dve", lambda e, ci=ci, m_=m_, b=b, T=T: e.tensor_scalar(
                                out=onT[b][:, :, 0:T], in0=cnd[ci][:, :, 0:T], scalar1=rk[:, m_:m_ + 1], scalar2=None, op0=ALU.mult),
                                reads=[cndB[ci], rkB], writes=[onTB[b]])
                        else:
                            P.op("dve", lambda e, ci=ci, m_=m_, b=b, T=T: e.scalar_tensor_tensor(
                                out=onT[b][:, :, 0:T], in0=cnd[ci][:, :, 0:T], scalar=rk[:, m_:m_ + 1], in1=onT[b][:, :, 0:T],
                                op0=ALU.mult, op1=ALU.add),
                                reads=[cndB[ci], rkB, onTB[b]], writes=[onTB[b]])
                xap, xm_ = io["x_src"](t0, T)
                P.dma("sp", xs[b][:, 0:T // 128, :], xap, writes=[xsB[b]])
                if xm_ is not None:
                    P.op("pool", lambda e, b=b, T=T, xm_=xm_: e.tensor_scalar(
                        out=xs[b][:, 0:T // 128, :], in0=xs[b][:, 0:T // 128, :], scalar1=rk[:, xm_:xm_ + 1], scalar2=None, op0=ALU.mult),
                        reads=[xsB[b], rkB], writes=[xsB[b]])

            load_grp(0)
            k = 0
            for gi, (t0, T) in enumerate(groups1):
                if gi + 1 < len(groups1):
                    load_grp(gi + 1)
                b = gi % 2
                for s in range(T // 128):
                    for half in range(2):
                        pi = k % 4
                        k += 1
                        for c in range(RC):
                            P.op("pe", lambda e, pi=pi, c=c, s=s, half=half, b=b: e.matmul(
                                py[pi][:, :], lhsT=onT[b][:, c, s * 128:(s + 1) * 128], rhs=wo[:, c, half * 512:(half + 1) * 512],
                                start=(c == 0), stop=(c == RC - 1)),
                                reads=[onTB[b], woB], writes=[pyB[pi]], mark=(c == RC - 1))
                        P.op("dve", lambda e, pi=pi, s=s, half=half, b=b: e.tensor_tensor(
                            out=xm[b][:, s, half * 512:(half + 1) * 512], in0=py[pi][:, :], in1=xs[b][:, s, half * 512:(half + 1) * 512], op=ALU.add),
                            reads=[pyB[pi], xsB[b]], writes=[xmsB[b]])
                xmB[t0] = C.buf()
                P.dma("pool", xm_d[t0:t0 + T, :].rearrange("(s p) d -> p s d", p=128), xm[b][:, 0:T // 128, :],
                      reads=[xmsB[b]], writes=[xmB[t0]])
            P.barrier()
            P.flush()

        with ExitStack() as es:
            gcol = C.sb(es, "gcol_sb", [128, 8], F32)
            gB = C.buf()
            P.dma("sp", gcol[:, :], g_d, writes=[gB])
            cw = C.sb(es, "cw_sb", [128, 2 * NFC, 3], F32)
            cbs = C.sb(es, "cb_sb", [128, 2 * NFC], F32)
            cwB = C.buf()
            P.dma("sp", cw[:, :, :], cw_d, writes=[cwB])
            P.dma("sp", cbs[:, :], cb_d, writes=[cwB])
            win = C.sb(es, "win_sb", [128, 8, 2 * DFF], BF16)
            wout = C.sb(es, "wout_sb", [128, NFC, D], BF16)
            winB, woutB = C.buf(), C.buf()
            with ExitStack() as es_st:
                load_weight_bf16(C, es_st, win_d, win, winB, 8, 2 * DFF, gcol, gB, colblk=DFF // 2, tag="win")
                load_weight_bf16(C, es_st, wout_d, wout, woutB, NFC, D, tag="wout")
                P.barrier()
                P.flush()
            winB.frozen = True
            woutB.frozen = True
            cwB.frozen = True
            if final:
                gf = C.sb(es, "gf_sb", [128, D], F32)
                gfB = C.buf()
                P.dma("sp", gf[:, :], gf_d, writes=[gfB])
                gfB.frozen = True
            TT = 256
            groups2 = [(0, 128)] + [(128 + i * TT, TT) for i in range(OWN // TT)]
            nrm = NormT(C, es, "nB")
            xmt = [C.sb(es, "xmt%d" % i, [128, 2, D], F32) for i in range(2)]
            xmtB = [C.buf() for _ in range(2)]
            hT = C.sb(es, "hTB", [128, 8, TT], BF16)
            hTB = C.buf()
            aT = C.sb(es, "aT", [128, NFC, TT], BF16)
            aTB = C.buf()
            NB2 = 2
            NBS = 2 if final else 3
            us = [[C.sb(es, "us%d_%d" % (w_, i), [128, TT + 2], F32) for i in range(NBS)] for w_ in range(2)]
            usB = [[C.buf() for _ in range(NBS)] for _ in range(2)]
            tc_ = [[C.sb(es, "tc%d_%d" % (w_, i), [128, TT], F32) for i in range(NBS)] for w_ in range(2)]
            tcB = [[C.buf() for _ in range(NBS)] for _ in range(2)]
            sg = [C.sb(es, "sg%d" % i, [128, TT], F32) for i in range(NBS)]
            sgB = [C.buf() for _ in range(NBS)]
            stash = C.sb(es, "stash", [128, 2 * NFC, 2], F32)
            stashB = [C.buf() for _ in range(2 * NFC)]
            xo = xmt
            xoB = xmtB
            P.op("pool", lambda e: e.memset(stash[:, :, :], 0.0), writes=stashB)
            pT = C.ps(es, "pTB", [128, 8, 128], BF16)
            pTB = C.buf()
            pu = [[C.ps(es, "pu%d_%d" % (w_, i), [128, TT], F32) for i in range(NB2)] for w_ in range(2)]
            puB = [[C.buf() for _ in range(NB2)] for _ in range(2)]
            py2 = [C.ps(es, "py2_%d" % i, [128, 512], F32) for i in range(2)]
            py2B = [C.buf() for _ in range(2)]
            if final:
                nf = NormT(C, es, "nF")
                fin = [C.sb(es, "fin%d" % i, [128, D], F32) for i in range(2)]
                finB = [C.buf() for _ in range(2)]

            def load_grp2(gi):
                t0, T = groups2[gi]
                b = gi % 2
                rd = [xmB[t] for t in xmB if t < t0 + T and t + (128 if t == 0 else 512) > t0]
                P.dma("sp", xmt[b][:, 0:T // 128, :], xm_d[t0:t0 + T, :].rearrange("(s p) d -> p s d", p=128),
                      reads=rd, writes=[xmtB[b]])

            load_grp2(0)
            kk = 0
            fk = 0
            pend = None
            for gi, (t0, T) in enumerate(groups2):
                if gi + 1 < len(groups2):
                    load_grp2(gi + 1)
                b = gi % 2
                ns = T // 128
                for s in range(ns):
                    si = nrm.rstd(xmt[b][:, s, :], xmtB[b])
                    nrm.normalize(si, [(xmt[b][:, s, :], D)], xmtB[b])
                    nrm.transpose_to(si, pT, pTB, lambda s=s: hT[:, :, s * 128:(s + 1) * 128], hTB)
                for i in range(NFC):
                    ub = kk % NB2
                    sb_ = kk % NBS
                    kk += 1
                    for w_ in range(2):
                        ch = i + NFC * w_
                        for c in range(8):
                            P.op("pe", lambda e, w_=w_, ub=ub, c=c, ch=ch, T=T: e.matmul(
                                pu[w_][ub][:, 0:T], lhsT=win[:, c, ch * 128:(ch + 1) * 128], rhs=hT[:, c, 0:T],
                                start=(c == 0), stop=(c == 7)),
                                reads=[winB, hTB], writes=[puB[w_][ub]], mark=(c == 7))
                        if gi == 0:
                            P.op("act", lambda e, w_=w_, ub=ub, ch=ch, T=T: e.activation(
                                out=stash[:, ch, :], in_=pu[w_][ub][:, T - 2:T], func=AF.Copy),
                                reads=[puB[w_][ub]], writes=[stashB[ch]])
                            continue
                        u_ = us[w_][sb_]
                        P.op("pool", lambda e, u_=u_, ch=ch: e.tensor_copy(out=u_[:, 0:2], in_=stash[:, ch, :]),
                             reads=[stashB[ch]], writes=[usB[w_][sb_]])
                        P.op("act", lambda e, u_=u_, w_=w_, ub=ub, T=T: e.activation(out=u_[:, 2:2 + T], in_=pu[w_][ub][:, 0:T], func=AF.Copy),
                             reads=[puB[w_][ub]], writes=[usB[w_][sb_]])
                        P.op("pool", lambda e, u_=u_, ch=ch, T=T: e.tensor_copy(out=stash[:, ch, :], in_=u_[:, T:T + 2]),
                             reads=[usB[w_][sb_]], writes=[stashB[ch]])
                        t_ = tc_[w_][sb_]
                        P.op("act", lambda e, w_=w_, ub=ub, t_=t_, ch=ch, T=T: e.activation(
                            out=t_[:, 0:T], in_=pu[w_][ub][:, 0:T], func=AF.Identity, scale=cw[:, ch, 2:3], bias=cbs[:, ch:ch + 1]),
                            reads=[puB[w_][ub], cwB], writes=[tcB[w_][sb_]])
                        for j in (1, 0):
                            P.op("dve", lambda e, u_=u_, t_=t_, ch=ch, T=T, j=j: e.scalar_tensor_tensor(
                                out=t_[:, 0:T], in0=u_[:, j:j + T], scalar=cw[:, ch, j:j + 1], in1=t_[:, 0:T], op0=ALU.mult, op1=ALU.add),
                                reads=[usB[w_][sb_], cwB, tcB[w_][sb_]], writes=[tcB[w_][sb_]])
                    if gi == 0:
                        continue

                    def gate(sb_, i, T):
                        P.op("act", lambda e: e.activation(out=sg[sb_][:, 0:T], in_=tc_[0][sb_][:, 0:T], func=AF.Silu),
                             reads=[tcB[0][sb_]], writes=[sgB[sb_]])
                        P.op("dve", lambda e: e.tensor_tensor(out=aT[:, i, 0:T], in0=sg[sb_][:, 0:T], in1=tc_[1][sb_][:, 0:T], op=ALU.mult),
                             reads=[sgB[sb_], tcB[1][sb_]], writes=[aTB])

                    if pend is not None:
                        gate(*pend)
                    pend = (sb_, i, T)
                if pend is not None:
                    gate(*pend)
                    pend = None
                if gi == 0:
                    continue
                for s in range(ns):
                    for half in range(2):
                        for i in range(NFC):
                            P.op("pe", lambda e, half=half, i=i, s=s: e.matmul(
                                py2[half][:, :], lhsT=aT[:, i, s * 128:(s + 1) * 128], rhs=wout[:, i, half * 512:(half + 1) * 512],
                                start=(i == 0), stop=(i == NFC - 1)),
                                reads=[aTB, woutB], writes=[py2B[half]], mark=(i == NFC - 1))
                        P.op("dve", lambda e, half=half, s=s, b=b: e.tensor_tensor(
                            out=xo[b][:, s, half * 512:(half + 1) * 512], in0=py2[half][:, :], in1=xmt[b][:, s, half * 512:(half + 1) * 512], op=ALU.add),
                            reads=[py2B[half], xmtB[b]], writes=[xoB[b]])
                    if final:
                        si = nf.rstd(xo[b][:, s, :], xoB[b])
                        fb = fk % 2
                        fk += 1
                        P.op("act", lambda e, fb=fb, b=b, s=s, si=si: e.activation(
                            out=fin[fb][:, :], in_=xo[b][:, s, :], func=AF.Copy, scale=nf.st[si][:, 1:2]),
                            reads=[xoB[b], nf.stB[si]], writes=[finB[fb]])
                        P.op("pool", lambda e, fb=fb: e.tensor_tensor(out=fin[fb][:, :], in0=fin[fb][:, :], in1=gf[:, :], op=ALU.mult),
                             reads=[finB[fb], gfB], writes=[finB[fb]])
                        r0 = t0 - 128 + s * 128
                        P.dma("sp", out_d[r0:r0 + 128, :], fin[fb][:, :], reads=[finB[fb]], writes=[C.buf()])
                if not final:
                    r0 = t0 - 128
                    P.dma("sp", out_d[r0:r0 + T, :].rearrange("(s p) d -> p s d", p=128), xo[b][:, 0:ns, :],
                          reads=[xoB[b]], writes=[C.buf()])
            P.barrier()
            P.flush()


def inputs_B(inp, layer, onT_b, xres_b, hh, consts):
    t0 = hh * OWN
    R = onT_b.shape[0]
    on = np.zeros((R, NTOK), onT_b.dtype)
    xx = np.zeros((NTOK, D), np.float32)
    if t0 > 0:
        on[:, 0:HALO] = onT_b[:, t0 - HALO:t0]
        xx[0:HALO] = xres_b[t0 - HALO:t0]
    on[:, HALO:] = onT_b[:, t0:t0 + OWN]
    xx[HALO:] = xres_b[t0:t0 + OWN]
    wo = np.asarray(inp["w_fox_out"][0] if layer == 0 else inp["w_mla_out"][0], np.float32)
    cw = np.asarray(inp["ffn_conv_w"][layer], np.float32)
    d = {
        "onT": on, "x": xx, "wo": np.ascontiguousarray(wo),
        "gcol": col128(inp["ffn_norm"][layer]),
        "w_in": np.ascontiguousarray(np.asarray(inp["w_ffn_in"][layer], np.float32)),
        "cw": np.ascontiguousarray(cw.T.reshape(2 * NFC, 128, 3).transpose(1, 0, 2)),
        "cb": col128(inp["ffn_conv_b"][layer]),
        "w_out": np.ascontiguousarray(np.asarray(inp["w_ffn_out"][layer], np.float32)),
    }
    if layer == 1:
        d["gfin"] = np.ascontiguousarray(np.broadcast_to(np.asarray(inp["final_norm"], np.float32)[None, :], (128, D)))
    d.update(consts)
    return d


MLA_SCALE = 192.0 ** -0.5


def build_C():
    nc = bass.Bass("TRN2", target_bir_lowering=False)
    with ExitStack() as es0:
        C = Ctx(nc, es0)
        io = dict(x=C.dram_in("x", [S, D], F32), wdkv=C.dram_in("wdkv", [D, 320], F32), gkv=C.dram_in("gkv", [128, 8], F32),
                  kvn=C.dram_in("kvn", [128, 2], F32), wuk=C.dram_in("wuk", [256, 1024], F32), wuv=C.dram_in("wuv", [256, 1024], F32),
                  gmla=C.dram_in("gmla", [128, 8], F32), wdq=C.dram_in("wdq", [D, 768], F32), qn=C.dram_in("qn", [128, 6], F32),
                  wuqn=C.dram_in("wuqn", [768, 1024], F32), wuqr=C.dram_in("wuqr", [768, 512], F32),
                  cos2=C.dram_in("cos2", [64, S], F32), sin2=C.dram_in("sin2", [64, S], F32),
                  out=C.dram_out("onT", [1024, S], BF16))
        load_consts(C, es0)
        emit_C(C, io)
    return nc


def emit_C(C, io):
    P = C.P
    C.pfx = "C_"
    with ExitStack() as es0:
        (x_d, wdkv_d, gkv_d, kvn_d, wuk_d, wuv_d, gmla_d, wdq_d, qn_d, wuqn_d, wuqr_d, cos_d, sin_d, out_d) = (
            io[k_] for k_ in ("x", "wdkv", "gkv", "kvn", "wuk", "wuv", "gmla", "wdq", "qn", "wuqn", "wuqr", "cos2", "sin2", "out"))
        QN = C.dram_scr("QN", [8, 128, S], BF16)
        QR = C.dram_scr("QR", [8, 64, S], BF16)
        KN = C.dram_scr("KN", [8, 128, S], BF16)
        KR = C.dram_scr("KR", [64, S], BF16)
        VS = C.dram_scr("VS", [8, 128, NT, 128], BF16)
        QNB = [[C.buf() for _ in range(NG)] for _ in range(8)]
        QRB = [[C.buf() for _ in range(NG)] for _ in range(8)]
        KNB = [[C.buf() for _ in range(NG)] for _ in range(8)]
        VSB = [[C.buf() for _ in range(NG)] for _ in range(8)]
        KRB = [C.buf() for _ in range(NG)]

        with ExitStack() as es:
            small = {}
            for nm, d_, n_ in (("gkv", gkv_d, 8), ("kvn", kvn_d, 2), ("gmla", gmla_d, 8), ("qn", qn_d, 6)):
                t_ = C.sb(es, nm + "_sb", [128, n_], F32)
                b_ = C.buf()
                P.dma("sp", t_[:, :], d_, writes=[b_])
                small[nm] = (t_, b_)
            wdkv = C.sb(es, "wdkv_sb", [128, 8, 320], BF16)
            wkrA = C.sb(es, "wkrA_sb", [128, 8, 128], BF16)
            wkrB = C.sb(es, "wkrB_sb", [128, 8, 128], BF16)
            wuk = C.sb(es, "wuk_sb", [128, 2, 1024], BF16)
            wuv = C.sb(es, "wuv_sb", [128, 2, 1024], BF16)
            wdq = C.sb(es, "wdq_sb", [128, 8, 768], BF16)
            wuqn = C.sb(es, "wuqn_sb", [128, 6, 1024], BF16)
            wuqr = C.sb(es, "wuqr_sb", [128, 6, 8, 64], BF16)
            wuqA = C.sb(es, "wuqA_sb", [128, 6, 8, 128], BF16)
            wuqB = C.sb(es, "wuqB_sb", [128, 6, 8, 128], BF16)
            wB = {k_: C.buf() for k_ in ("dkv", "dkvs", "uk", "uv", "dq", "uqn", "uqr", "uqs")}
            with ExitStack() as es_st:
                load_weight_bf16(C, es_st, wdkv_d, wdkv, wB["dkv"], 8, 320, *small["gkv"], tag="wdkv")
                load_weight_bf16(C, es_st, wuk_d, wuk, wB["uk"], 2, 1024, *small["kvn"], tag="wuk")
                load_weight_bf16(C, es_st, wuv_d, wuv, wB["uv"], 2, 1024, *small["kvn"], tag="wuv")
                load_weight_bf16(C, es_st, wdq_d, wdq, wB["dq"], 8, 768, *small["gmla"], tag="wdq")
                load_weight_bf16(C, es_st, wuqn_d, wuqn, wB["uqn"], 6, 1024, *small["qn"], tag="wuqn")
                load_weight_bf16(C, es_st, wuqr_d, wuqr[:, :, :, :].rearrange("p l h d -> p l (h d)"), wB["uqr"], 6, 512, *small["qn"], tag="wuqr")
                def neg(dst, src, rd, wr):
                    P.op("dve", lambda e: e.tensor_scalar(out=dst, in0=src, scalar1=-1.0, scalar2=None, op0=ALU.mult), reads=[rd], writes=[wr])

                def cpy(dst, src, rd, wr):
                    P.op("dve", lambda e: e.tensor_copy(out=dst, in_=src), reads=[rd], writes=[wr])

                cpy(wkrA[:, :, 0:64], wdkv[:, :, 256:320], wB["dkv"], wB["dkvs"])
                neg(wkrA[:, :, 64:96], wdkv[:, :, 288:320], wB["dkv"], wB["dkvs"])
                cpy(wkrA[:, :, 96:128], wdkv[:, :, 256:288], wB["dkv"], wB["dkvs"])
                neg(wkrB[:, :, 0:32], wdkv[:, :, 288:320], wB["dkv"], wB["dkvs"])
                cpy(wkrB[:, :, 32:64], wdkv[:, :, 256:288], wB["dkv"], wB["dkvs"])
                cpy(wkrB[:, :, 64:128], wdkv[:, :, 256:320], wB["dkv"], wB["dkvs"])
                for l in range(6):
                    cpy(wuqA[:, l, :, 0:64], wuqr[:, l, :, :], wB["uqr"], wB["uqs"])
                    neg(wuqA[:, l, :, 64:96], wuqr[:, l, :, 32:64], wB["uqr"], wB["uqs"])
                    cpy(wuqA[:, l, :, 96:128], wuqr[:, l, :, 0:32], wB["uqr"], wB["uqs"])
                    neg(wuqB[:, l, :, 0:32], wuqr[:, l, :, 32:64], wB["uqr"], wB["uqs"])
                    cpy(wuqB[:, l, :, 32:64], wuqr[:, l, :, 0:32], wB["uqr"], wB["uqs"])
                    cpy(wuqB[:, l, :, 64:128], wuqr[:, l, :, :], wB["uqr"], wB["uqs"])
                if io.get("pre_main") is not None:
                    io["pre_main"]()
                P.barrier()
                P.flush()
            for b_ in wB.values():
                b_.frozen = True
            nrm = NormT(C, es, "nC")
            nL = NormT(C, es, "nL", width=256)
            nQ = NormT(C, es, "nQ", width=768)
            xt = [C.sb(es, "xtC%d" % i, [128, D], F32) for i in range(3)]
            xtB = [C.buf() for _ in range(3)]
            hT = [C.sb(es, "hTC%d" % i, [128, 8, 512], BF16) for i in range(2)]
            hTB = [C.buf() for _ in range(2)]
            latT = C.sb(es, "latT", [128, 2, 512], BF16)
            latTB = C.buf()
            cqT = C.sb(es, "cqT", [128, 6, 512], BF16)
            cqTB = C.buf()
            KNst = C.sb(es, "KNst", [128, 8, 512], BF16)
            QNst = C.sb(es, "QNst", [128, 8, 512], BF16)
            QRst = C.sb(es, "QRst", [64, 8, 512], BF16)
            Vst = C.sb(es, "VstC", [128, 8, 4, 128], BF16)
            KRst = C.sb(es, "KRst", [64, 512], BF16)
            KNstB, QNstB, QRstB, VstB, KRstB = C.buf(), C.buf(), C.buf(), C.buf(), C.buf()
            cs = [[C.sb(es, "cs%d_%d" % (j, i), [64, 512], F32) for i in range(2)] for j in range(2)]
            csB = [C.buf() for _ in range(2)]
            t1 = [C.sb(es, "ropet1_%d" % i, [64, 512], F32) for i in range(2)]
            t2 = [C.sb(es, "ropet2_%d" % i, [64, 512], F32) for i in range(2)]
            t1B = [C.buf() for _ in range(2)]
            t2B = [C.buf() for _ in range(2)]
            pT = C.ps(es, "pTC", [128, 8, 128], BF16)
            pTB = C.buf()
            pp = [C.ps(es, "ppC%d" % i, [128, 512], F32) for i in range(6)]
            ppB = [C.buf() for _ in range(6)]
            prr = [0]

            def nextp():
                i = prr[0] % 6
                prr[0] += 1
                return pp[i], ppB[i]

            rk = [0]

            def rope(pa, paB, pb_, pbB, cb_, dst_ap, dstB):
                i = rk[0] % 2
                rk[0] += 1
                P.op("dve", lambda e, i=i: e.tensor_tensor(out=t1[i][:, :], in0=pa[0:64, :], in1=cs[0][cb_][:, :], op=ALU.mult),
                     reads=[paB, csB[cb_]], writes=[t1B[i]])
                P.op("dve", lambda e, i=i: e.tensor_tensor(out=t2[i][:, :], in0=pb_[0:64, :], in1=cs[1][cb_][:, :], op=ALU.mult),
                     reads=[pbB, csB[cb_]], writes=[t2B[i]])
                P.op("pool", lambda e, i=i: e.tensor_tensor(out=dst_ap, in0=t1[i][:, :], in1=t2[i][:, :], op=ALU.add),
                     reads=[t1B[i], t2B[i]], writes=[dstB])

            xv = x_d.rearrange("(n p) d -> n p d", p=128)

            def load_x(ti):
                i = ti % 3
                P.dma("sp", xt[i][:, :], xv[ti], writes=[xtB[i]])

            def load_cs(tg):
                b = tg % 2
                P.dma("sp", cs[0][b][:, :], cos_d[:, tg * 512:(tg + 1) * 512], writes=[csB[b]])
                P.dma("sp", cs[1][b][:, :], sin_d[:, tg * 512:(tg + 1) * 512], writes=[csB[b]])

            load_x(0)
            load_x(1)
            load_cs(0)
            ev = [0]

            def evac(out_ap, in_ap, inB, outB):
                ev[0] += 1
                if ev[0] % 2:
                    P.op("act", lambda e: e.activation(out=out_ap, in_=in_ap, func=AF.Copy), reads=[inB], writes=[outB])
                else:
                    P.op("dve", lambda e: e.tensor_copy(out=out_ap, in_=in_ap), reads=[inB], writes=[outB])

            import os
            PARTS = os.environ.get('C1_PARTS', 'lat,kr,kn,v,q,qh').split(',')
            for tg in range(int(os.environ.get('C1_NG', NG))):
                hb = tg % 2
                cb_ = tg % 2
                if tg + 1 < NG:
                    load_cs(tg + 1)
                for s in range(4):
                    ti = tg * 4 + s
                    if ti + 2 < NT:
                        load_x(ti + 2)
                    xi = ti % 3
                    si = nrm.rstd(xt[xi][:, :], xtB[xi])
                    nrm.normalize(si, [(xt[xi][:, :], D)], xtB[xi])
                    nrm.transpose_to(si, pT, pTB, lambda hb=hb, s=s: hT[hb][:, :, s * 128:(s + 1) * 128], hTB[hb])
                h_ = hT[hb]
                hB_ = hTB[hb]
                for s in (range(4) if 'lat' in PARTS else []):
                    pc, pcB = nextp()
                    for c in range(8):
                        P.op("pe", lambda e, pc=pc, c=c, s=s, h_=h_: e.matmul(pc[:, 0:256], lhsT=h_[:, c, s * 128:(s + 1) * 128], rhs=wdkv[:, c, 0:256],
                                                                      start=(c == 0), stop=(c == 7)),
                             reads=[hB_, wB["dkv"]], writes=[pcB], mark=(c == 7))
                    si = nL.rstd(None, pcB, parts=[(pc[:, 0:256], 256)])
                    nL.normalize(si, [(pc[:, 0:256], 256)], pcB)
                    nL.transpose_to(si, pT, pTB, lambda s=s: latT[:, :, s * 128:(s + 1) * 128], latTB)
                pk, pkB = nextp()
                pks, pksB = nextp()
                for (dst, dB, w_ap) in () if 'kr' not in PARTS else ((pk, pkB, lambda c: wkrA[:, c, :]), (pks, pksB, lambda c: wkrB[:, c, :])):
                    for c in range(8):
                        P.op("pe", lambda e, dst=dst, c=c, w_ap=w_ap, h_=h_: e.matmul(dst[:, :], lhsT=w_ap(c), rhs=h_[:, c, :], start=(c == 0), stop=(c == 7)),
                             reads=[hB_, wB["dkv"], wB["dkvs"]], writes=[dB], mark=(c == 7))
                if 'kr' in PARTS:
                    rope(pk, pkB, pks, pksB, cb_, KRst[:, :], KRstB)
                    P.dma("pool", KR[:, tg * 512:(tg + 1) * 512], KRst[:, :], reads=[KRstB], writes=[KRB[tg]])
                for h in (range(8) if 'kn' in PARTS else []):
                    pn, pnB = nextp()
                    for l in range(2):
                        P.op("pe", lambda e, pn=pn, l=l, h=h: e.matmul(pn[:, :], lhsT=wuk[:, l, h * 128:(h + 1) * 128], rhs=latT[:, l, :], start=(l == 0), stop=(l == 1)),
                             reads=[latTB, wB["uk"]], writes=[pnB], mark=(l == 1))
                    evac(KNst[:, h, :], pn[:, :], pnB, KNstB)
                if 'kn' in PARTS:
                    P.dma("pool", KN[:, :, tg * 512:(tg + 1) * 512].rearrange("h p t -> p h t"), KNst[:, :, :], reads=[KNstB],
                          writes=[KNB[h_i][tg] for h_i in range(8)])
                for s in (range(4) if 'v' in PARTS else []):
                    for half in range(2):
                        pv_, pvB_ = nextp()
                        for l in range(2):
                            P.op("pe", lambda e, pv_=pv_, l=l, s=s, half=half: e.matmul(
                                pv_[:, :], lhsT=latT[:, l, s * 128:(s + 1) * 128], rhs=wuv[:, l, half * 512:(half + 1) * 512], start=(l == 0), stop=(l == 1)),
                                reads=[latTB, wB["uv"]], writes=[pvB_], mark=(l == 1))
                        evac(Vst[:, half * 4:(half + 1) * 4, s, :], pv_[:, :].rearrange("p (h d) -> p h d", d=128), pvB_, VstB)
                for h in (range(8) if 'v' in PARTS else []):
                    P.dma("pool", VS[h, :, tg * 4:(tg + 1) * 4, :], Vst[:, h, :, :], reads=[VstB], writes=[VSB[h][tg]])
                for s in (range(4) if 'q' in PARTS else []):
                    pa, paB = nextp()
                    pb2, pb2B = nextp()
                    for (dst, dB, c0, c1) in ((pa, paB, 0, 512), (pb2, pb2B, 512, 768)):
                        for c in range(8):
                            P.op("pe", lambda e, dst=dst, c=c, s=s, c0=c0, c1=c1, h_=h_: e.matmul(
                                dst[:, 0:c1 - c0], lhsT=h_[:, c, s * 128:(s + 1) * 128], rhs=wdq[:, c, c0:c1], start=(c == 0), stop=(c == 7)),
                                reads=[hB_, wB["dq"]], writes=[dB], mark=(c == 7))
                    jb = C.buf()
                    si = nQ.rstd2([(pa[:, 0:512], 512, paB), (pb2[:, 0:256], 256, pb2B)])
                    nQ.normalize2(si, [(pa[:, 0:512], 512, paB), (pb2[:, 0:256], 256, pb2B)])
                    nQ.transpose_to(si, pT, pTB, lambda s=s: cqT[:, :, s * 128:(s + 1) * 128], cqTB)
                for h in (range(8) if 'qh' in PARTS else []):
                    pn, pnB = nextp()
                    for l in range(6):
                        P.op("pe", lambda e, pn=pn, l=l, h=h: e.matmul(pn[:, :], lhsT=wuqn[:, l, h * 128:(h + 1) * 128], rhs=cqT[:, l, :], start=(l == 0), stop=(l == 5)),
                             reads=[cqTB, wB["uqn"]], writes=[pnB], mark=(l == 5))
                    evac(QNst[:, h, :], pn[:, :], pnB, QNstB)
                    pr, prB = nextp()
                    prs, prsB = nextp()
                    for (dst, dB, w_t) in ((pr, prB, wuqA), (prs, prsB, wuqB)):
                        for l in range(6):
                            P.op("pe", lambda e, dst=dst, l=l, h=h, w_t=w_t: e.matmul(dst[:, :], lhsT=w_t[:, l, h, :], rhs=cqT[:, l, :], start=(l == 0), stop=(l == 5)),
                                 reads=[cqTB, wB["uqr"], wB["uqs"]], writes=[dB], mark=(l == 5))
                    rope(pr, prB, prs, prsB, cb_, QRst[:, h, :], QRstB)
                if 'qh' in PARTS:
                    P.dma("pool", QN[:, :, tg * 512:(tg + 1) * 512].rearrange("h p t -> p h t"), QNst[:, :, :], reads=[QNstB],
                          writes=[QNB[h_i][tg] for h_i in range(8)])
                    P.dma("pool", QR[:, :, tg * 512:(tg + 1) * 512].rearrange("h p t -> p h t"), QRst[:, :, :], reads=[QRstB],
                          writes=[QRB[h_i][tg] for h_i in range(8)])
            P.barrier()
            P.flush()

        with ExitStack() as es:
            QNh = [C.sb(es, "QNh%d" % i, [128, S], BF16) for i in range(2)]
            QRh = [C.sb(es, "QRh%d" % i, [128, S], BF16) for i in range(2)]
            KNh = [C.sb(es, "KNh%d" % i, [128, S], BF16) for i in range(2)]
            Vh = [C.sb(es, "VhC%d" % i, [128, NT, 128], BF16) for i in range(2)]
            KRs = C.sb(es, "KRs", [128, S], BF16)
            QNhB = [C.buf() for _ in range(2)]
            QRhB = [C.buf() for _ in range(2)]
            KNhB = [C.buf() for _ in range(2)]
            VhB = [C.buf() for _ in range(2)]
            KRsB = C.buf()
            P.op("pool", lambda e: e.memset(KRs[64:128, :], 0.0), writes=[KRsB])
            for i_ in range(2):
                P.op("pool", lambda e, i_=i_: e.memset(QRh[i_][64:128, :], 0.0), writes=[QRhB[i_]])
            P.dma("sp", KRs[0:64, :], KR, reads=KRB, writes=[KRsB])
            KRsB.frozen = True

            def load_head(h):
                hb = h % 2
                P.dma("sp", QNh[hb][:, :], QN[h], reads=QNB[h], writes=[QNhB[hb]])
                P.dma("sp", QRh[hb][0:64, :], QR[h], reads=QRB[h], writes=[QRhB[hb]])
                P.dma("sp", KNh[hb][:, :], KN[h], reads=KNB[h], writes=[KNhB[hb]])
                P.dma("sp", Vh[hb][:, :, :], VS[h], reads=VSB[h], writes=[VhB[hb]])

            def kparts(h):
                hb = h % 2
                parts = [(lambda kt, hb=hb: KNh[hb][:, kt * 128:(kt + 1) * 128], lambda a, b, hb=hb: QNh[hb][:, a:b]),
                         (lambda kt: KRs[:, kt * 128:(kt + 1) * 128], lambda a, b, hb=hb: QRh[hb][:, a:b])]
                return parts, [KNhB[hb], QNhB[hb], QRhB[hb], KRsB], (lambda kt, hb=hb: Vh[hb][:, kt, :]), VhB[hb]

            import os
            if not os.environ.get("SKIP_C2"):
                attention_phase(C, es, 8, 128, kparts, load_head, out_d, MLA_SCALE, C.mask_mla, lambda h, kt: None, "sep", after_head=io.get("after_head"))
            P.barrier()
            P.flush()


def _rstd2(self, parts):
    P = self.C.P
    i = self.k % 2
    self.k += 1
    st, stB = self.st[i], self.stB[i]
    off = 0
    for j, (ap, w, B_) in enumerate(parts):
        P.op("act", lambda e, ap=ap, w=w, off=off, j=j: e.activation(
            out=self.junk[:, off:off + w], in_=ap, func=AF.Square, accum_out=st[:, 2 + j:3 + j]),
            reads=[B_], writes=[self.junkB, stB])
        off += w
    if len(parts) == 2:
        P.op("dve", lambda e: e.tensor_tensor(out=st[:, 2:3], in0=st[:, 2:3], in1=st[:, 3:4], op=ALU.add), reads=[stB], writes=[stB])
    P.op("act", lambda e: e.activation(out=st[:, 0:1], in_=st[:, 2:3], func=AF.Sqrt, scale=1.0 / self.width, bias=EPS), reads=[stB], writes=[stB])
    P.op("dve", lambda e: e.reciprocal(out=st[:, 1:2], in_=st[:, 0:1]), reads=[stB], writes=[stB])
    return i


def _normalize2(self, i, parts):
    P = self.C.P
    st, stB = self.st[i], self.stB[i]
    off = 0
    for (ap, w, B_) in parts:
        P.op("act", lambda e, ap=ap, w=w, off=off: e.activation(out=self.xn[i][:, off:off + w], in_=ap, func=AF.Copy, scale=st[:, 1:2]),
             reads=[B_, stB], writes=[self.xnB[i]])
        off += w


NormT.rstd2 = _rstd2
NormT.normalize2 = _normalize2


def rope_tables():
    pos = np.arange(S, dtype=np.float32)
    inv = (np.float32(10000.0) ** (-np.arange(0, 64, 2, dtype=np.float32) / np.float32(64))).astype(np.float32)
    ang = pos[:, None] * inv[None, :]
    cos = np.cos(ang).astype(np.float32).T
    sin = np.sin(ang).astype(np.float32).T
    return np.ascontiguousarray(np.concatenate([cos, cos], 0)), np.ascontiguousarray(np.concatenate([sin, sin], 0))


def inputs_C(inp, x1_b, hh, consts, tabs):
    wuq = np.asarray(inp["w_uq"][0], np.float32).reshape(768, 16, 192)[:, hh * 8:(hh + 1) * 8, :]
    d = {
        "x": np.ascontiguousarray(x1_b, dtype=np.float32),
        "wdkv": np.ascontiguousarray(np.asarray(inp["w_dkv"], np.float32)),
        "gkv": col128(inp["kv_in_norm"]),
        "kvn": col128(inp["kv_norm"]),
        "wuk": np.ascontiguousarray(np.asarray(inp["w_uk"], np.float32)[:, hh * 1024:(hh + 1) * 1024]),
        "wuv": np.ascontiguousarray(np.asarray(inp["w_uv"], np.float32)[:, hh * 1024:(hh + 1) * 1024]),
        "gmla": col128(inp["mla_norm"][0]),
        "wdq": np.ascontiguousarray(np.asarray(inp["w_dq"][0], np.float32)),
        "qn": col128(inp["q_norm"][0]),
        "wuqn": np.ascontiguousarray(wuq[:, :, 0:128].reshape(768, 1024)),
        "wuqr": np.ascontiguousarray(wuq[:, :, 128:192].reshape(768, 512)),
        "cos2": tabs[0], "sin2": tabs[1],
    }
    d.update(consts)
    return d


def exchange(C, groups, src, dst_fn, nchunk, rows, slots, standalone=True):
    P = C.P
    snd, rcv, sndB, rcvB = slots
    ns = len(snd)
    if standalone:
        P.barrier()

    def put(i):
        k = i % ns
        P.dma("sp", snd[k], src[i * rows:(i + 1) * rows, :], writes=[sndB[k]])

    for i in range(min(ns, nchunk)):
        put(i)
    for i in range(nchunk):
        k = i % ns
        P.op("pool", lambda e, k=k: e.collective_compute("AllGather", ALU.bypass, replica_groups=groups, ins=[snd[k]], outs=[rcv[k]]),
             reads=[sndB[k]], writes=[rcvB[k]])
        for r in range(2):
            P.dma("sp", dst_fn(r, i), rcv[k][r * rows:(r + 1) * rows, :], reads=[rcvB[k]], writes=[C.buf()])
        if i + ns < nchunk:
            put(i + ns)
    if standalone:
        P.barrier()
        P.flush()


def make_head_exchange(C, groups, snd_all, dst_fn, heads_per_chunk, slots):
    P = C.P
    snd, rcv, sndB, rcvB = slots
    ns = len(snd)

    def hook(h, outB):
        if (h + 1) % heads_per_chunk:
            return
        i = h // heads_per_chunk
        k = i % ns
        rd = [b for hh in range(h + 1 - heads_per_chunk, h + 1) for b in outB[hh]]
        P.dma("pool", snd[k], snd_all[i * 128:(i + 1) * 128, :], reads=rd, writes=[sndB[k]])
        P.op("pool", lambda e, k=k: e.collective_compute("AllGather", ALU.bypass, replica_groups=groups, ins=[snd[k]], outs=[rcv[k]]),
             reads=[sndB[k]], writes=[rcvB[k]])
        for r in range(2):
            P.dma("pool", dst_fn(r, i), rcv[k][r * 128:(r + 1) * 128, :], reads=[rcvB[k]], writes=[C.buf()])
    return hook


def build_fused(n_cores=8):
    nc = bass.Bass("TRN2", target_bir_lowering=False)
    groups = [[2 * i, 2 * i + 1] for i in range(n_cores // 2)]
    with ExitStack() as es0:
        C = Ctx(nc, es0)
        P = C.P
        load_consts(C, es0)
        rk_d = C.dram_in("rk", [128, 4], F32)
        rk = C.sb(es0, "rk_sb", [128, 4], F32)
        rkB = C.buf()
        P.dma("sp", rk[:, :], rk_d, writes=[rkB])
        rkB.frozen = True
        tok = lambda ap: ap.rearrange("(s p) d -> p s d", p=128)
        NSLOT = 2
        sl16 = ([C.dram_scr("cc_s16_%d" % i, [128, S], BF16) for i in range(NSLOT)], [C.dram_scr("cc_r16_%d" % i, [256, S], BF16) for i in range(NSLOT)],
                [C.buf() for _ in range(NSLOT)], [C.buf() for _ in range(NSLOT)])
        sl32 = ([C.dram_scr("cc_s32_%d" % i, [512, D], F32) for i in range(NSLOT)], [C.dram_scr("cc_r32_%d" % i, [1024, D], F32) for i in range(NSLOT)],
                [C.buf() for _ in range(NSLOT)], [C.buf() for _ in range(NSLOT)])

        on0_snd = C.dram_scr("on0_snd", [512, S], BF16)
        on0_all = C.dram_scr("on0_all", [1024, S], BF16)
        emit_A(C, dict(x=C.dram_in("xb", [S, D], F32), wq=C.dram_in("wq", [D, 512], F32), wk=C.dram_in("wk", [D, 512], F32),
                       wv=C.dram_in("wv", [D, 512], F32), wf=C.dram_in("wf", [D, 8], F32), bf=C.dram_in("bf", [128, 8], F32),
                       g=C.dram_in("gA", [128, 8], F32), out=on0_snd,
                       after_head=make_head_exchange(C, groups, on0_snd, lambda r, i: on0_all[r * 512 + i * 128:r * 512 + (i + 1) * 128, :], 2, sl16)))
        import os
        STOP = int(os.environ.get("FUSED_STOP", 99))
        if STOP >= 2:
            pass

        x_own = C.dram_in("x_own", [NTOK, D], F32)
        x1_snd = C.dram_scr("x1_snd", [OWN, D], F32)
        x1_all = C.dram_scr("x1_all", [S, D], F32)

        def mk_cands(all_ap):
            v = all_ap.rearrange("(c p) t -> p c t", p=128)

            def f(t0, T):
                if t0 == 0:
                    return [(v[:, :, OWN - HALO:OWN], 1)]
                r = t0 - HALO
                return [(v[:, :, r:r + T], 0), (v[:, :, OWN + r:OWN + r + T], 1)]
            return f

        def ffn_io(l, extra):
            d = dict(wo=C.dram_in("wo%d" % l, [(8 if l == 0 else 16) * 128, D], F32), g=C.dram_in("g%d" % l, [128, 8], F32),
                     w_in=C.dram_in("w_in%d" % l, [D, 2 * DFF], F32), cw=C.dram_in("cw%d" % l, [128, 2 * NFC, 3], F32),
                     cb=C.dram_in("cb%d" % l, [128, 2 * NFC], F32), w_out=C.dram_in("w_out%d" % l, [DFF, D], F32), rk=rk, rkB=rkB)
            d.update(extra)
            return d

        io0 = ffn_io(0, dict(
            out=x1_snd, on_cands=mk_cands(on0_all), x_src=lambda t0, T: (tok(x_own[t0:t0 + T, :]), None)))
        if STOP >= 3:
            emit_B(C, 8, False, "L0_", io0)
        if STOP >= 4:
            pass

        on1_snd = C.dram_scr("on1_snd", [1024, S], BF16)
        on1_all = C.dram_scr("on1_all", [2048, S], BF16)
        ioC = (dict(x=x1_all, wdkv=C.dram_in("wdkv", [D, 320], F32), gkv=C.dram_in("gkv", [128, 8], F32),
                       kvn=C.dram_in("kvn", [128, 2], F32), wuk=C.dram_in("wuk", [256, 1024], F32), wuv=C.dram_in("wuv", [256, 1024], F32),
                       gmla=C.dram_in("gmla", [128, 8], F32), wdq=C.dram_in("wdq", [D, 768], F32), qn=C.dram_in("qn", [128, 6], F32),
                       wuqn=C.dram_in("wuqn", [768, 1024], F32), wuqr=C.dram_in("wuqr", [768, 512], F32),
                       cos2=C.dram_in("cos2", [64, S], F32), sin2=C.dram_in("sin2", [64, S], F32), out=on1_snd,
                       after_head=make_head_exchange(C, groups, on1_snd, lambda r, i: on1_all[r * 1024 + i * 128:r * 1024 + (i + 1) * 128, :], 1, sl16),
                       pre_main=lambda: exchange(C, groups, x1_snd, lambda r, i: x1_all[r * OWN + i * 512:r * OWN + (i + 1) * 512, :], 8, 512, sl32,
                                                 standalone=False)))
        if STOP >= 5:
            emit_C(C, ioC)
        if STOP >= 6:
            pass

        out_d = C.dram_out("out", [OWN, D], F32)

        def x_src1(t0, T):
            if t0 == 0:
                return (tok(x1_all[OWN - HALO:OWN, :]), 1)
            r = t0 - HALO
            return (tok(x1_snd[r:r + T, :]), None)

        io1 = ffn_io(1, dict(
            out=out_d, gfin=C.dram_in("gfin", [128, D], F32), on_cands=mk_cands(on1_all), x_src=x_src1))
        if STOP >= 7:
            emit_B(C, 16, True, "L1_", io1)
    return nc


def inputs_fused(inp, b, hh, consts, tabs):
    x = np.asarray(inp["x"], np.float32)
    dA = inputs_A(inp, b, hh, consts)
    d = {"xb": dA["x"], "wq": dA["wq"], "wk": dA["wk"], "wv": dA["wv"], "wf": dA["wf"], "bf": dA["bf"], "gA": dA["gcol"]}
    xo = np.zeros((NTOK, D), np.float32)
    t0 = hh * OWN
    if t0 > 0:
        xo[0:HALO] = x[b, t0 - HALO:t0]
    xo[HALO:] = x[b, t0:t0 + OWN]
    d["x_own"] = xo
    for l in range(2):
        cw = np.asarray(inp["ffn_conv_w"][l], np.float32)
        d["wo%d" % l] = np.ascontiguousarray(np.asarray(inp["w_fox_out"][0] if l == 0 else inp["w_mla_out"][0], np.float32))
        d["g%d" % l] = col128(inp["ffn_norm"][l])
        d["w_in%d" % l] = np.ascontiguousarray(np.asarray(inp["w_ffn_in"][l], np.float32))
        d["cw%d" % l] = np.ascontiguousarray(cw.T.reshape(2 * NFC, 128, 3).transpose(1, 0, 2))
        d["cb%d" % l] = col128(inp["ffn_conv_b"][l])
        d["w_out%d" % l] = np.ascontiguousarray(np.asarray(inp["w_ffn_out"][l], np.float32))
    d["gfin"] = np.ascontiguousarray(np.broadcast_to(np.asarray(inp["final_norm"], np.float32)[None, :], (128, D)))
    dC = inputs_C(inp, x[b], hh, consts, tabs)
    for k_ in ("wdkv", "gkv", "kvn", "wuk", "wuv", "gmla", "wdq", "qn", "wuqn", "wuqr", "cos2", "sin2"):
        d[k_] = dC[k_]
    rk = np.zeros((128, 4), np.float32)
    rk[:, hh] = 1.0
    d["rk"] = rk
    d.update(consts)
    return d


def kernel(**inp):
    cst = host_consts()
    tabs = rope_tables()
    cores = list(range(8))
    nc = build_fused(8)
    maps = [inputs_fused(inp, c // 2, c % 2, cst, tabs) for c in cores]
    res = run_bass_kernel_spmd(nc, maps, core_ids=cores).results
    out = np.stack([np.concatenate([np.asarray(res[2 * b]["out"]), np.asarray(res[2 * b + 1]["out"])], 0) for b in range(NB)], 0)
    return out.astype(np.float32)
```

```python
import numpy as np
import ml_dtypes
from contextlib import ExitStack
import concourse.bass as bass
import concourse.mybir as mybir
from concourse.bass_utils import run_bass_kernel_spmd

F32 = mybir.dt.float32
BF16 = mybir.dt.bfloat16
AF = mybir.ActivationFunctionType
ALU = mybir.AluOpType

D = 1024
S = 8192
NB = 4
NT = S // 128
NG = S // 512
DFF = 2816
NFC = DFF // 128
EPS = 1e-6
NEG = -30000.0
OWN = S // 2
HALO = 128
NTOK = OWN + HALO


class Buf:
    __slots__ = ("name", "w", "r", "frozen")

    def __init__(self, name):
        self.name = name
        self.w = None
        self.r = {}
        self.frozen = False


class Rec:
    ENG = ("pe", "act", "dve", "pool", "sp")
    NLANES = {"sp": 8, "pool": 6, "act": 2}

    def __init__(self, nc, sems):
        self.nc = nc
        self.sems = sems
        self.cnt = {k: 0 for k in sems}
        self.seen = {e: {} for e in self.ENG}
        self.stream = {e: [] for e in self.ENG}
        self.lane_rr = {q: 0 for q in self.NLANES}
        self.log = None

    def _need(self, eng, tok):
        if tok is None:
            return
        key, val = tok
        if val <= 0:
            return
        if key == eng and eng == "pe":
            return
        if self.seen[eng].get(key, 0) >= val:
            return
        self.seen[eng][key] = val
        self.stream[eng].append(("wait", key, val))

    def _deps(self, eng, reads, writes, extra=()):
        for b in reads:
            self._need(eng, b.w)
        for b in writes:
            if b.w is not None and b.w[0] != eng:
                self._need(eng, b.w)
            for k, v in b.r.items():
                if k != eng:
                    self._need(eng, (k, v))
        for t in extra:
            self._need(eng, t)

    def _post(self, tok, reads, writes):
        k, v = tok
        for b in reads:
            if not b.frozen and b.r.get(k, 0) < v:
                b.r[k] = v
        for b in writes:
            b.w = tok
            b.r = {}

    def op(self, eng, fn, reads=(), writes=(), mark=True, extra=()):
        self._deps(eng, reads, writes, extra)
        if mark:
            self.cnt[eng] += 1
            tok = (eng, self.cnt[eng])
            self.stream[eng].append(("op", fn, eng, 1))
        else:
            tok = (eng, self.cnt[eng] + 1)
            self.stream[eng].append(("op", fn, None, 0))
        self._post(tok, reads, writes)
        return tok

    def dma(self, q, out, in_, reads=(), writes=(), extra=()):
        n = self.NLANES[q]
        lane = "%s_l%d" % (q, self.lane_rr[q] % n)
        self.lane_rr[q] += 1
        self._deps(q, reads, writes, extra)
        self._need(q, (lane, self.cnt[lane]))
        self.cnt[lane] += 16
        tok = (lane, self.cnt[lane])
        self.stream[q].append(("op", lambda e, o=out, i=in_: e.dma_start(out=o, in_=i), lane, 16))
        self._post(tok, reads, writes)
        return tok

    def barrier(self):
        for e in self.ENG:
            for k in self.cnt:
                self._need(e, (k, self.cnt[k]))

    def flush(self):
        nc = self.nc
        streams = self.stream
        self.stream = {e: [] for e in self.ENG}
        sems = self.sems
        if self.log is not None:
            self.log.append(streams)
            return

        def run(eng_obj, items):
            for it in items:
                if it[0] == "wait":
                    eng_obj.wait_ge(sems[it[1]], it[2])
                else:
                    ins = it[1](eng_obj)
                    if it[2] is not None:
                        ins.then_inc(sems[it[2]], it[3])

        with nc.Block() as block:
            if streams["pe"]:
                @block.tensor
                def _(e):
                    run(e, streams["pe"])
            if streams["act"]:
                @block.scalar
                def _(e):
                    run(e, streams["act"])
            if streams["dve"]:
                @block.vector
                def _(e):
                    run(e, streams["dve"])
            if streams["pool"]:
                @block.gpsimd
                def _(e):
                    run(e, streams["pool"])
            if streams["sp"]:
                @block.sync
                def _(e):
                    run(e, streams["sp"])


def sem_keys():
    keys = list(Rec.ENG)
    for q, n in Rec.NLANES.items():
        keys += ["%s_l%d" % (q, i) for i in range(n)]
    return keys


class Ctx:
    def __init__(self, nc, es):
        self.nc = nc
        self.es = es
        sems = {k: es.enter_context(nc.semaphore(k)) for k in sem_keys()}
        self.P = Rec(nc, sems)
        self.nbuf = 0
        self.pfx = ""

    def sb(self, es, name, shape, dt):
        return es.enter_context(self.nc.sbuf_tensor(self.pfx + name, shape, dt))

    def ps(self, es, name, shape, dt):
        return es.enter_context(self.nc.psum_tensor(self.pfx + name, shape, dt))

    def buf(self, name=None):
        self.nbuf += 1
        return Buf(name or ("b%d" % self.nbuf))

    def dram_in(self, name, shape, dt):
        return self.nc.dram_tensor(name, list(shape), dt, kind="ExternalInput").ap()

    def dram_out(self, name, shape, dt):
        return self.nc.dram_tensor(name, list(shape), dt, kind="ExternalOutput").ap()

    def dram_scr(self, name, shape, dt):
        return self.nc.dram_tensor(self.pfx + name, list(shape), dt).ap()


def load_consts(C, es):
    P = C.P
    C.cb_d = C.dram_in("cst_bf", [128, 512], BF16)
    C.cf_d = C.dram_in("cst_f", [128, 384], F32)
    C.cb = C.sb(es, "cst_bf_sb", [128, 512], BF16)
    C.cf = C.sb(es, "cst_f_sb", [128, 384], F32)
    C.cbB = C.buf("cb")
    C.cfB = C.buf("cf")
    P.dma("sp", C.cb[:, :], C.cb_d, writes=[C.cbB])
    P.dma("sp", C.cf[:, :], C.cf_d, writes=[C.cfB])
    C.cbB.frozen = True
    C.cfB.frozen = True
    C.ident = C.cb[:, 0:128]
    C.mask_fox = C.cb[:, 128:256]
    C.mask_mla = C.cb[:, 256:384]
    C.ones_bf = C.cb[:, 384:512]
    C.identf = C.cf[:, 0:128]
    C.utri = C.cf[:, 128:256]
    C.onesf = C.cf[:, 256:384]


def host_consts():
    cb = np.zeros((128, 512), np.float32)
    cb[:, 0:128] = np.eye(128)
    p = np.arange(128)[:, None]
    c = np.arange(128)[None, :]
    cb[:, 128:256] = np.where(p <= c, 0.0, NEG)
    cb[:, 256:384] = np.where((p >= 64) & (c < 64), NEG, 0.0)
    cb[:, 384:512] = 1.0
    cf = np.zeros((128, 384), np.float32)
    cf[:, 0:128] = np.eye(128)
    cf[:, 128:256] = (p <= c).astype(np.float32)
    cf[:, 256:384] = 1.0
    return {"cst_bf": cb.astype(ml_dtypes.bfloat16), "cst_f": cf}


def load_weight_bf16(C, es_stage, w_d, dst, dstB, nchunk, ncol, gcol=None, gB=None, colblk=None, tag="w"):
    P = C.P
    colblk = colblk or ncol
    NSTG = 3
    stg = [C.sb(es_stage, "%s_stg%d" % (tag, i), [128, colblk], F32) for i in range(NSTG)]
    stgB = [C.buf() for _ in range(NSTG)]
    wv = w_d.rearrange("(c p) f -> p c f", p=128)
    k = 0
    for c in range(nchunk):
        for c0 in range(0, ncol, colblk):
            i = k % NSTG
            k += 1
            P.dma("sp", stg[i][:, :], wv[:, c, c0:c0 + colblk], writes=[stgB[i]])
            eng = ("dve", "act", "pool", "dve", "act")[k % 5]
            if eng == "act":
                if gcol is not None:
                    P.op("act", lambda e, i=i, c=c, c0=c0: e.activation(
                        out=dst[:, c, c0:c0 + colblk], in_=stg[i][:, :], func=AF.Copy, scale=gcol[:, c:c + 1]),
                        reads=[stgB[i]] + ([gB] if gB else []), writes=[dstB])
                else:
                    P.op("act", lambda e, i=i, c=c, c0=c0: e.activation(out=dst[:, c, c0:c0 + colblk], in_=stg[i][:, :], func=AF.Copy),
                         reads=[stgB[i]], writes=[dstB])
            elif gcol is not None:
                P.op(eng, lambda e, i=i, c=c, c0=c0: e.tensor_scalar(
                    out=dst[:, c, c0:c0 + colblk], in0=stg[i][:, :], scalar1=gcol[:, c:c + 1], scalar2=None, op0=ALU.mult),
                    reads=[stgB[i]] + ([gB] if gB else []), writes=[dstB])
            else:
                P.op(eng, lambda e, i=i, c=c, c0=c0: e.tensor_copy(out=dst[:, c, c0:c0 + colblk], in_=stg[i][:, :]),
                     reads=[stgB[i]], writes=[dstB])


class NormT:
    def __init__(self, C, es, tag, width=D):
        self.C = C
        self.width = width
        self.nch = width // 128
        self.junk = C.sb(es, tag + "_junk", [128, width], F32)
        self.junkB = C.buf()
        self.st = [C.sb(es, tag + "_st%d" % i, [128, 4], F32) for i in range(2)]
        self.stB = [C.buf() for _ in range(2)]
        self.xn = [C.sb(es, tag + "_xn%d" % i, [128, width], BF16) for i in range(2)]
        self.xnB = [C.buf() for _ in range(2)]
        self.k = 0

    def rstd(self, src_ap, srcB, parts=None):
        C, P = self.C, self.C.P
        i = self.k % 2
        self.k += 1
        st, stB = self.st[i], self.stB[i]
        parts = parts or [(src_ap, self.width)]
        off = 0
        for j, (ap, w) in enumerate(parts):
            P.op("act", lambda e, ap=ap, w=w, off=off, j=j: e.activation(
                out=self.junk[:, off:off + w], in_=ap, func=AF.Square, accum_out=st[:, 2 + j:3 + j]),
                reads=[srcB], writes=[self.junkB, stB])
            off += w
        if len(parts) == 2:
            P.op("dve", lambda e: e.tensor_tensor(out=st[:, 2:3], in0=st[:, 2:3], in1=st[:, 3:4], op=ALU.add),
                 reads=[stB], writes=[stB])
        P.op("act", lambda e: e.activation(out=st[:, 0:1], in_=st[:, 2:3], func=AF.Sqrt, scale=1.0 / self.width, bias=EPS),
             reads=[stB], writes=[stB])
        P.op("dve", lambda e: e.reciprocal(out=st[:, 1:2], in_=st[:, 0:1]), reads=[stB], writes=[stB])
        return i

    def normalize(self, i, parts, srcB):
        P = self.C.P
        st, stB = self.st[i], self.stB[i]
        off = 0
        for (ap, w) in parts:
            P.op("act", lambda e, ap=ap, w=w, off=off: e.activation(
                out=self.xn[i][:, off:off + w], in_=ap, func=AF.Copy, scale=st[:, 1:2]),
                reads=[srcB, stB], writes=[self.xnB[i]])
            off += w
        return self.xn[i], self.xnB[i]

    def transpose_to(self, i, pT, pTB, dst_fn, dstB):
        C, P = self.C, self.C.P
        for c in range(self.nch):
            P.op("pe", lambda e, c=c: e.transpose(out=pT[:, c, :], in_=self.xn[i][:, c * 128:(c + 1) * 128], identity=C.ident),
                 reads=[self.xnB[i], C.cbB], writes=[pTB], mark=(c == self.nch - 1))
        P.op("dve", lambda e: e.tensor_copy(out=dst_fn(), in_=pT[:, 0:self.nch, :]), reads=[pTB], writes=[dstB])


def attention_phase(C, es, n_heads, dv, kparts, load_head, out_d, scale, mask_ap, bias_fn, den_mode, after_head=None):
    P = C.P
    NPS = 5 if den_mode == "row64" else 4
    ps = [C.ps(es, "att_ps%d" % i, [128, 512], F32) for i in range(NPS)]
    psB = [C.buf() for _ in range(NPS)]
    po = [C.ps(es, "att_po%d" % i, [128, 512], F32) for i in range(2)]
    poB = [C.buf() for _ in range(2)]
    if den_mode == "sep":
        pl = [C.ps(es, "att_pl%d" % i, [128, 512], F32) for i in range(2)]
        plB = [C.buf() for _ in range(2)]
        acc = [C.sb(es, "att_acc%d" % i, [128, 512], F32) for i in range(2)]
        accB = [C.buf() for _ in range(2)]
    else:
        pb = C.ps(es, "att_pb", [128, 512], F32)
        pbB = C.buf()
    pt = [C.sb(es, "att_pt%d" % i, [128, 512], BF16) for i in range(NPS)]
    ptB = [C.buf() for _ in range(NPS)]
    rl = C.sb(es, "att_rl", [128, 512], F32)
    rlB = C.buf()
    bc = C.sb(es, "att_bc", [128, 512], F32)
    bcB = C.buf()
    on = [C.sb(es, "att_on%d" % i, [128, 512], BF16) for i in range(2)]
    onB = [C.buf() for _ in range(2)]
    drow = dv if den_mode == "row64" else 0
    P.op("pool", lambda e: e.memset(rl[:, :], 0.0), writes=[rlB])

    gcount = [0]
    ucount = [0]
    outB = {}
    load_head(0)
    for h in range(n_heads):
        if h + 1 < n_heads:
            load_head(h + 1)
        hb = h % 2
        parts, partB, v_fn, vB = kparts(h)
        units = [(g, kt) for g in range(NG) for kt in range(4 * (g + 1))]
        n = len(units)
        LOOK = NPS - 1
        state = {}

        def emit_qk(u, idx):
            g, kt = u
            i = kt - 4 * g
            c0 = 128 * i if i >= 0 else 0
            b = idx % NPS
            np_ = len(parts)
            for pi, (kf, qf) in enumerate(parts):
                last = (pi == np_ - 1) and (i < 0)
                P.op("pe", lambda e, kf=kf, qf=qf, pi=pi, last=last, b=b, c0=c0, g=g, kt=kt: e.matmul(
                    ps[b][:, c0:512], lhsT=kf(kt), rhs=qf(g * 512 + c0, (g + 1) * 512), start=(pi == 0), stop=last,
                    skip_group_check=True),
                    reads=partB, writes=[psB[b]], mark=last)
            if i >= 0:
                P.op("pe", lambda e, b=b, c0=c0: e.matmul(ps[b][:, c0:c0 + 128], lhsT=C.ident, rhs=mask_ap,
                                                          start=False, stop=True, skip_group_check=True),
                     reads=[C.cbB], writes=[psB[b]], mark=True)

        def emit_rest(u, idx):
            g, kt = u
            i = kt - 4 * g
            c0 = 128 * i if i >= 0 else 0
            b = idx % NPS
            nkt = 4 * (g + 1)
            gi = state.setdefault(g, None)
            if kt == 0:
                state[g] = gcount[0] % 2
                gcount[0] += 1
            ob = state[g]
            bias = bias_fn(h, kt)
            if bias is not None:
                bias_ap, biasB = bias
                P.op("act", lambda e, b=b, c0=c0, bias_ap=bias_ap: e.activation(
                    out=pt[b][:, c0:512], in_=ps[b][:, c0:512], func=AF.Exp, bias=bias_ap, scale=scale),
                    reads=[psB[b], biasB], writes=[ptB[b]])
            else:
                P.op("act", lambda e, b=b, c0=c0: e.activation(
                    out=pt[b][:, c0:512], in_=ps[b][:, c0:512], func=AF.Exp, scale=scale),
                    reads=[psB[b]], writes=[ptB[b]])
            mrows = dv + 1 if den_mode == "row64" else dv
            P.op("pe", lambda e, b=b, c0=c0, kt=kt, ob=ob, nkt=nkt, mrows=mrows, vf=v_fn: e.matmul(
                po[ob][0:mrows, c0:512], lhsT=vf(kt), rhs=pt[b][:, c0:512], start=(kt == 0), stop=(kt == nkt - 1),
                skip_group_check=True),
                reads=[ptB[b], vB], writes=[poB[ob]], mark=True)
            if den_mode == "sep":
                if kt == 0:
                    P.op("dve", lambda e, b=b, ob=ob: e.tensor_copy(out=acc[ob][:, :], in_=pt[b][:, :]),
                         reads=[ptB[b]], writes=[accB[ob]])
                elif kt % 4 != 1:
                    P.op("dve", lambda e, b=b, ob=ob, c0=c0: e.tensor_tensor(
                        out=acc[ob][:, c0:512], in0=acc[ob][:, c0:512], in1=pt[b][:, c0:512], op=ALU.add),
                        reads=[ptB[b], accB[ob]], writes=[accB[ob]])
                else:
                    P.op("pe", lambda e, b=b, c0=c0, kt=kt, ob=ob: e.matmul(
                        pl[ob][:, c0:512], lhsT=C.ones_bf[:, 0:128], rhs=pt[b][:, c0:512], start=(kt == 1), stop=False,
                        skip_group_check=True),
                        reads=[ptB[b], C.cbB], writes=[plB[ob]], mark=True)
            if kt == nkt - 1:
                oi = ob
                if den_mode == "sep":
                    P.op("pe", lambda e, ob=ob: e.matmul(pl[ob][:, :], lhsT=C.onesf[:, 0:128], rhs=acc[ob][:, :], start=False, stop=True,
                                                         skip_group_check=True),
                         reads=[accB[ob], C.cfB], writes=[plB[ob]], mark=True)
                    P.op("dve", lambda e, ob=ob: e.reciprocal(out=bc[:, :], in_=pl[ob][:, :]), reads=[plB[ob]], writes=[bcB])
                else:
                    P.op("dve", lambda e, ob=ob: e.reciprocal(out=rl[drow:drow + 1, :], in_=po[ob][drow:drow + 1, :]),
                         reads=[poB[ob]], writes=[rlB])
                    P.op("pe", lambda e: e.matmul(pb[:, :], lhsT=C.onesf[:, 0:128], rhs=rl[:, :],
                                                  start=True, stop=True, skip_group_check=True),
                         reads=[rlB, C.cfB], writes=[pbB], mark=True)
                    P.op("dve", lambda e: e.tensor_copy(out=bc[0:dv, :], in_=pb[0:dv, :]), reads=[pbB], writes=[bcB])
                P.op("dve", lambda e, ob=ob, oi=oi: e.tensor_tensor(out=on[oi][0:dv, :], in0=po[ob][0:dv, :], in1=bc[0:dv, :], op=ALU.mult),
                     reads=[poB[ob], bcB], writes=[onB[oi]])
                ob_ = C.buf()
                outB.setdefault(h, []).append(ob_)
                P.dma("sp", out_d[h * dv:(h + 1) * dv, g * 512:(g + 1) * 512], on[oi][0:dv, :], reads=[onB[oi]], writes=[ob_])

        for i in range(n + LOOK):
            if i < n:
                emit_qk(units[i], ucount[0] + i)
            j = i - LOOK
            if j >= 0:
                emit_rest(units[j], ucount[0] + j)
        ucount[0] += n
        if after_head is not None:
            after_head(h, outB)


def build_A():
    nc = bass.Bass("TRN2", target_bir_lowering=False)
    with ExitStack() as es0:
        C = Ctx(nc, es0)
        io = dict(x=C.dram_in("x", [S, D], F32), wq=C.dram_in("wq", [D, 512], F32), wk=C.dram_in("wk", [D, 512], F32),
                  wv=C.dram_in("wv", [D, 512], F32), wf=C.dram_in("wf", [D, 8], F32), bf=C.dram_in("bf", [128, 8], F32),
                  g=C.dram_in("gcol", [128, 8], F32), out=C.dram_out("onT", [512, S], BF16))
        load_consts(C, es0)
        emit_A(C, io)
    return nc


def emit_A(C, io):
    P = C.P
    C.pfx = "A_"
    with ExitStack() as es0:
        x_d, wq_d, wk_d, wv_d, wf_d, bf_d, g_d, out_d = (io[k_] for k_ in ("x", "wq", "wk", "wv", "wf", "bf", "g", "out"))
        Qs = C.dram_scr("Qs", [8, 67, S], BF16)
        Ks = C.dram_scr("Ks", [8, 64, S], BF16)
        Vs = C.dram_scr("Vs", [8, 128, NT, 65], BF16)
        QsB = [[C.buf() for _ in range(NG)] for _ in range(8)]
        KsB = [[C.buf() for _ in range(NG)] for _ in range(8)]
        VsB = [[C.buf() for _ in range(NG)] for _ in range(8)]
        cK = C.sb(es0, "cK", [128, NT, 8], F32)
        cKB = C.buf()

        with ExitStack() as es:
            gcol = C.sb(es, "gcol_sb", [128, 8], F32)
            gB = C.buf()
            P.dma("sp", gcol[:, :], g_d, writes=[gB])
            bfs = C.sb(es, "bf_sb", [128, 8], F32)
            bfB = C.buf()
            P.dma("sp", bfs[:, :], bf_d, writes=[bfB])
            wq = C.sb(es, "wq_sb", [128, 8, 576], BF16)
            wk = C.sb(es, "wk_sb", [128, 8, 576], BF16)
            wv = C.sb(es, "wv_sb", [128, 8, 512], BF16)
            wf = C.sb(es, "wf_sb", [128, 8, 8], BF16)
            wqB, wkB, wvB, wfB = C.buf(), C.buf(), C.buf(), C.buf()
            if True:
                es_st = es
                P.op("pool", lambda e: e.memset(wq[:, :, 512:576], 0.0), writes=[wqB])
                P.op("pool", lambda e: e.memset(wk[:, :, 512:576], 0.0), writes=[wkB])
                load_weight_bf16(C, es_st, wq_d, wq, wqB, 8, 512, gcol, gB, tag="wq")
                load_weight_bf16(C, es_st, wk_d, wk, wkB, 8, 512, gcol, gB, tag="wk")
                load_weight_bf16(C, es_st, wv_d, wv, wvB, 8, 512, gcol, gB, tag="wv")
                load_weight_bf16(C, es_st, wf_d, wf, wfB, 8, 8, gcol, gB, tag="wf")
                for b_ in (wqB, wkB, wvB, wfB):
                    b_.frozen = True
                nrm = NormT(C, es, "nA")
                xt = [C.sb(es, "xtA%d" % i, [128, D], F32) for i in range(3)]
                xtB = [C.buf() for _ in range(3)]
                hT = [C.sb(es, "hTA%d" % i, [128, 8, 512], BF16) for i in range(2)]
                hTB = [C.buf() for _ in range(2)]
                Qst = [C.sb(es, "Qst%d" % i, [64, 8, 512], BF16) for i in range(2)]
                Kst = [C.sb(es, "Kst%d" % i, [64, 8, 512], BF16) for i in range(2)]
                Vst = [C.sb(es, "Vst%d" % i, [128, 8, 4, 65], BF16) for i in range(2)]
                QstB = [C.buf() for _ in range(2)]
                KstB = [C.buf() for _ in range(2)]
                VstB = [C.buf() for _ in range(2)]
                lall = C.sb(es, "lall", [128, NT, 8], F32)
                lallB = C.buf()
                for i in range(2):
                    P.op("pool", lambda e, i=i: e.memset(Vst[i][:, :, :, 64:65], 1.0), writes=[VstB[i]])
                pT = C.ps(es, "pTA", [128, 8, 128], BF16)
                pTB = C.buf()
                pq = [C.ps(es, "pqA%d" % i, [128, 512], F32) for i in range(2)]
                pqB = [C.buf() for _ in range(2)]
                pv = [C.ps(es, "pvA%d" % i, [128, 512], F32) for i in range(2)]
                pvB = [C.buf() for _ in range(2)]
                pz = C.ps(es, "pzA", [128, 8], F32)
                pzB = C.buf()
                xv = x_d.rearrange("(n p) d -> n p d", p=128)
                nld = [0]

                def load_x(tile_idx):
                    i = tile_idx % 3
                    P.dma("sp", xt[i][:, :], xv[tile_idx], writes=[xtB[i]])

                load_x(0)
                load_x(1)
                qk = 0
                for tg in range(NG):
                    hb = tg % 2
                    for s in range(4):
                        ti = tg * 4 + s
                        if ti + 2 < NT:
                            load_x(ti + 2)
                        xi = ti % 3
                        si = nrm.rstd(xt[xi][:, :], xtB[xi])
                        nrm.normalize(si, [(xt[xi][:, :], D)], xtB[xi])
                        nrm.transpose_to(si, pT, pTB, lambda hb=hb, s=s: hT[hb][:, :, s * 128:(s + 1) * 128], hTB[hb])
                    for which, (w_sb, wB, st_, stB_) in enumerate(((wq, wqB, Qst, QstB), (wk, wkB, Kst, KstB))):
                        for h in range(8):
                            pi = qk % 2
                            qk += 1
                            for c in range(8):
                                P.op("pe", lambda e, pi=pi, c=c, h=h, w_sb=w_sb, hb=hb: e.matmul(
                                    pq[pi][:, :], lhsT=w_sb[:, c, h * 64:h * 64 + 128], rhs=hT[hb][:, c, :],
                                    start=(c == 0), stop=(c == 7)),
                                    reads=[wB, hTB[hb]], writes=[pqB[pi]], mark=(c == 7))
                            eng = "act" if (h % 2 == 1) else "dve"
                            if eng == "act":
                                P.op("act", lambda e, pi=pi, h=h, st_=st_, hb=hb: e.activation(
                                    out=st_[hb][0:64, h, :], in_=pq[pi][0:64, :], func=AF.Copy),
                                    reads=[pqB[pi]], writes=[stB_[hb]])
                            else:
                                P.op("dve", lambda e, pi=pi, h=h, st_=st_, hb=hb: e.tensor_copy(
                                    out=st_[hb][0:64, h, :], in_=pq[pi][0:64, :]),
                                    reads=[pqB[pi]], writes=[stB_[hb]])
                    P.dma("pool", Qs[:, 0:64, tg * 512:(tg + 1) * 512].rearrange("h p t -> p h t"), Qst[hb][:, :, :],
                          reads=[QstB[hb]], writes=[QsB[h_][tg] for h_ in range(8)])
                    P.dma("pool", Ks[:, :, tg * 512:(tg + 1) * 512].rearrange("h p t -> p h t"), Kst[hb][:, :, :],
                          reads=[KstB[hb]], writes=[KsB[h_][tg] for h_ in range(8)])
                    for s in range(4):
                        pi = s % 2
                        for c in range(8):
                            P.op("pe", lambda e, pi=pi, c=c, s=s, hb=hb: e.matmul(
                                pv[pi][:, :], lhsT=hT[hb][:, c, s * 128:(s + 1) * 128], rhs=wv[:, c, :],
                                start=(c == 0), stop=(c == 7)),
                                reads=[wvB, hTB[hb]], writes=[pvB[pi]], mark=(c == 7))
                        P.op("act", lambda e, pi=pi, s=s, hb=hb: e.activation(
                            out=Vst[hb][:, :, s, 0:64], in_=pv[pi][:, :].rearrange("p (h d) -> p h d", d=64), func=AF.Copy),
                            reads=[pvB[pi]], writes=[VstB[hb]])
                        for c in range(8):
                            P.op("pe", lambda e, c=c, s=s, hb=hb: e.matmul(
                                pz[:, :], lhsT=hT[hb][:, c, s * 128:(s + 1) * 128], rhs=wf[:, c, :],
                                start=(c == 0), stop=(c == 7)),
                                reads=[wfB, hTB[hb]], writes=[pzB], mark=(c == 7))
                        P.op("dve", lambda e, s=s, tg=tg: e.tensor_tensor(
                            out=lall[:, tg * 4 + s, :], in0=pz[:, :], in1=bfs[:, :], op=ALU.add),
                            reads=[pzB, bfB], writes=[lallB])
                    for h in range(8):
                        P.dma("pool", Vs[h, :, tg * 4:(tg + 1) * 4, :], Vst[hb][:, h, :, :],
                              reads=[VstB[hb]], writes=[VsB[h][tg]])

                lf = lall[:, :, :].rearrange("p j h -> p (j h)")
                P.op("act", lambda e: e.activation(out=lf, in_=lf, func=AF.Exp, scale=-1.0), reads=[lallB], writes=[lallB])
                P.op("act", lambda e: e.activation(out=lf, in_=lf, func=AF.Ln, bias=1.0), reads=[lallB], writes=[lallB])
                pw = pq[0]
                ptot = pq[1]
                P.op("pe", lambda e: e.matmul(pw[:, :], lhsT=C.utri, rhs=lf, start=True, stop=True),
                     reads=[lallB, C.cfB], writes=[pqB[0]])
                P.op("pe", lambda e: e.matmul(ptot[:, :], lhsT=C.onesf, rhs=lf, start=True, stop=True),
                     reads=[lallB, C.cfB], writes=[pqB[1]])
                sc = [C.sb(es, "scan%d" % i, [128, NT, 8], F32) for i in range(2)]
                scB = [C.buf() for _ in range(2)]
                P.op("act", lambda e: e.activation(out=sc[0][:, :, :].rearrange("p j h -> p (j h)"), in_=ptot[:, :], func=AF.Copy),
                     reads=[pqB[1]], writes=[scB[0]])
                cur = 0
                dd = 1
                while dd < NT:
                    nxt = 1 - cur
                    P.op("dve", lambda e, cur=cur, nxt=nxt, dd=dd: e.tensor_copy(out=sc[nxt][:, 0:dd, :], in_=sc[cur][:, 0:dd, :]),
                         reads=[scB[cur]], writes=[scB[nxt]])
                    P.op("dve", lambda e, cur=cur, nxt=nxt, dd=dd: e.tensor_tensor(
                        out=sc[nxt][:, dd:NT, :], in0=sc[cur][:, dd:NT, :], in1=sc[cur][:, 0:NT - dd, :], op=ALU.add),
                        reads=[scB[cur]], writes=[scB[nxt]])
                    cur = nxt
                    dd *= 2
                P.op("dve", lambda e: e.tensor_copy(out=cK[:, 0:1, :], in_=pw[:, 0:8].rearrange("p (j h) -> p j h", h=8)),
                     reads=[pqB[0]], writes=[cKB])
                P.op("dve", lambda e, cur=cur: e.tensor_tensor(
                    out=cK[:, 1:NT, :], in0=pw[:, 8:NT * 8].rearrange("p (j h) -> p j h", h=8), in1=sc[cur][:, 0:NT - 1, :], op=ALU.add),
                    reads=[pqB[0], scB[cur]], writes=[cKB])
                c8 = C.sb(es, "c8", [128, NT * 8], F32)
                t32 = C.sb(es, "t32", [128, NT * 8], F32)
                c8B, t32B = C.buf(), C.buf()
                pcs = [C.sb(es, "pcs%d" % i, [128, NT, 8], BF16) for i in range(3)]
                pcsB = [C.buf() for _ in range(3)]
                P.op("act", lambda e: e.activation(out=c8[:, :], in_=cK[:, :, :].rearrange("p j h -> p (j h)"), func=AF.Copy, scale=-8.0),
                     reads=[cKB], writes=[c8B])
                for i in range(3):
                    P.op("dve", lambda e, i=i: e.tensor_copy(out=pcs[i][:, :, :].rearrange("p j h -> p (j h)"), in_=c8[:, :]),
                         reads=[c8B], writes=[pcsB[i]])
                    if i < 2:
                        P.op("dve", lambda e, i=i: e.tensor_copy(out=t32[:, :], in_=pcs[i][:, :, :].rearrange("p j h -> p (j h)")),
                             reads=[pcsB[i]], writes=[t32B])
                        P.op("dve", lambda e: e.tensor_tensor(out=c8[:, :], in0=c8[:, :], in1=t32[:, :], op=ALU.subtract),
                             reads=[c8B, t32B], writes=[c8B])
                pTc = C.ps(es, "pTc", [128, 4, 512], BF16)
                pTcB = C.buf()
                cst_ = [C.sb(es, "cstg%d" % i, [8, 3, 512], BF16) for i in range(2)]
                cstB = [C.buf() for _ in range(2)]
                for r in range(NG):
                    for i in range(3):
                        for s_ in range(4):
                            P.op("pe", lambda e, r=r, i=i, s_=s_: e.transpose(
                                out=pTc[0:8, i, s_ * 128:(s_ + 1) * 128], in_=pcs[i][:, r * 4 + s_, :], identity=C.ident),
                                reads=[pcsB[i], C.cbB], writes=[pTcB], mark=(i == 2 and s_ == 3))
                    P.op("dve", lambda e, r=r: e.tensor_copy(out=cst_[r % 2][:, :, :], in_=pTc[0:8, 0:3, :]),
                         reads=[pTcB], writes=[cstB[r % 2]])
                    P.dma("pool", Qs[:, 64:67, r * 512:(r + 1) * 512], cst_[r % 2][:, :, :], reads=[cstB[r % 2]],
                          writes=[QsB[h_][r] for h_ in range(8)])
                cKB.frozen = True
                P.barrier()
                P.flush()

        with ExitStack() as es:
            Qh = [C.sb(es, "Qh%d" % i, [67, S], BF16) for i in range(2)]
            Kh = [C.sb(es, "Kh%d" % i, [67, S], BF16) for i in range(2)]
            Vh = [C.sb(es, "Vh%d" % i, [128, NT, 65], BF16) for i in range(2)]
            QhB = [C.buf() for _ in range(2)]
            KhB = [C.buf() for _ in range(2)]
            VhB = [C.buf() for _ in range(2)]
            for i in range(2):
                P.op("pool", lambda e, i=i: e.memset(Kh[i][64:67, :], 1.0), writes=[KhB[i]])

            def load_head(h):
                hb = h % 2
                P.dma("sp", Qh[hb][:, :], Qs[h], reads=QsB[h], writes=[QhB[hb]])
                P.dma("sp", Kh[hb][0:64, :], Ks[h], reads=KsB[h], writes=[KhB[hb]])
                P.dma("sp", Vh[hb][:, :, :], Vs[h], reads=VsB[h], writes=[VhB[hb]])

            def kparts(h):
                hb = h % 2
                parts = [(lambda kt, hb=hb: Kh[hb][:, kt * 128:(kt + 1) * 128],
                          lambda a, b, hb=hb: Qh[hb][:, a:b])]
                return parts, [KhB[hb], QhB[hb]], (lambda kt, hb=hb: Vh[hb][:, kt, :]), VhB[hb]

            def bias_fn(h, kt):
                return cK[:, kt, h:h + 1], cKB

            attention_phase(C, es, 8, 64, kparts, load_head, out_d, 0.125, C.mask_fox, bias_fn, "row64", after_head=io.get("after_head"))
            P.barrier()
            P.flush()


def col128(v):
    v = np.asarray(v, np.float32)
    return np.ascontiguousarray(v.reshape(-1, 128).T)


def inputs_A(inp, b, hh, consts):
    w = np.asarray(inp["w_fox_in"][0])
    hs = slice(hh * 512, (hh + 1) * 512)
    d = {
        "x": np.ascontiguousarray(np.asarray(inp["x"][b], np.float32)),
        "wq": np.ascontiguousarray(w[:, 0:1024][:, hs]),
        "wk": np.ascontiguousarray(w[:, 1024:2048][:, hs]),
        "wv": np.ascontiguousarray(w[:, 2048:3072][:, hs]),
        "wf": np.ascontiguousarray(w[:, 3072 + hh * 8:3072 + (hh + 1) * 8]),
        "bf": np.ascontiguousarray(np.broadcast_to(np.asarray(inp["b_fox_f"][0], np.float32)[hh * 8:(hh + 1) * 8][None, :], (128, 8))),
        "gcol": col128(inp["fox_norm"][0]),
    }
    d.update(consts)
    return d


def build_B(RC, final):
    nc = bass.Bass("TRN2", target_bir_lowering=False)
    with ExitStack() as es0:
        C = Ctx(nc, es0)
        onT_d = C.dram_in("onT", [RC * 128, NTOK], BF16)
        x_d = C.dram_in("x", [NTOK, D], F32)
        onv = onT_d.rearrange("(c p) t -> p c t", p=128)
        io = dict(wo=C.dram_in("wo", [RC * 128, D], F32), g=C.dram_in("gcol", [128, 8], F32),
                  w_in=C.dram_in("w_in", [D, 2 * DFF], F32), cw=C.dram_in("cw", [128, 2 * NFC, 3], F32),
                  cb=C.dram_in("cb", [128, 2 * NFC], F32), w_out=C.dram_in("w_out", [DFF, D], F32),
                  out=C.dram_out("xo", [OWN, D], F32),
                  on_cands=lambda t0, T: [(onv[:, :, t0:t0 + T], None)],
                  x_src=lambda t0, T: (x_d[t0:t0 + T, :].rearrange("(s p) d -> p s d", p=128), None))
        if final:
            io["gfin"] = C.dram_in("gfin", [128, D], F32)
        load_consts(C, es0)
        emit_B(C, RC, final, "L%d_" % int(final), io)
    return nc


def emit_B(C, RC, final, pfx, io):
    P = C.P
    C.pfx = pfx
    with ExitStack() as es0:
        wo_d, g_d, win_d, cw_d, cb_d, wout_d, out_d = (io[k_] for k_ in ("wo", "g", "w_in", "cw", "cb", "w_out", "out"))
        if final:
            gf_d = io["gfin"]
        xm_d = C.dram_scr("xm", [NTOK, D], F32)
        rk = io.get("rk")
        rkB = io.get("rkB")
        groups1 = [(0, 128)] + [(128 + i * 512, 512) for i in range(OWN // 512)]
        xmB = {}

        with ExitStack() as es:
            wo = C.sb(es, "wo_sb", [128, RC, D], BF16)
            woB = C.buf()
            load_weight_bf16(C, es, wo_d, wo, woB, RC, D, tag="wo")
            woB.frozen = True
            onT = [C.sb(es, "onT%d" % i, [128, RC, 512], BF16) for i in range(2)]
            onTB = [C.buf() for _ in range(2)]
            xs = [C.sb(es, "xs%d" % i, [128, 4, D], F32) for i in range(2)]
            xsB = [C.buf() for _ in range(2)]
            xm = [C.sb(es, "xm%d" % i, [128, 4, D], F32) for i in range(2)]
            xmsB = [C.buf() for _ in range(2)]
            py = [C.ps(es, "pyB%d" % i, [128, 512], F32) for i in range(4)]
            pyB = [C.buf() for _ in range(4)]
            ncand = max(len(io["on_cands"](t0_, T_)) for (t0_, T_) in groups1)
            blend = any(m_ is not None for (t0_, T_) in groups1 for (_, m_) in io["on_cands"](t0_, T_))
            if blend:
                cnd = [C.sb(es, "cnd%d" % i, [128, RC, 512], BF16) for i in range(ncand)]
                cndB = [C.buf() for _ in range(ncand)]

            def load_grp(gi):
                t0, T = groups1[gi]
                b = gi % 2
                cands = io["on_cands"](t0, T)
                if not blend:
                    P.dma("sp", onT[b][:, :, 0:T], cands[0][0], writes=[onTB[b]])
                else:
                    for ci, (ap, m_) in enumerate(cands):
                        P.dma("sp", cnd[ci][:, :, 0:T], ap, writes=[cndB[ci]])
                    for ci, (ap, m_) in enumerate(cands):
                        if ci == 0:
                            P.op("dve", lambda e, ci=ci, m_=m_, b=b, T=T: e.tensor_scalar(
                                out=onT[b][:, :, 0:T], in0=cnd[ci][:, :, 0:T], scalar1=rk[:, m_:m_ + 1], scalar2=None, op0=ALU.mult),
                                reads=[cndB[ci], rkB], writes=[onTB[b]])
                        else:
                            P.op("dve", lambda e, ci=ci, m_=m_, b=b, T=T: e.scalar_tensor_tensor(
                                out=onT[b][:, :, 0:T], in0=cnd[ci][:, :, 0:T], scalar=rk[:, m_:m_ + 1], in1=onT[b][:, :, 0:T],
                                op0=ALU.mult, op1=ALU.add),
                                reads=[cndB[ci], rkB, onTB[b]], writes=[onTB[b]])
                xap, xm_ = io["x_src"](t0, T)
                P.dma("sp", xs[b][:, 0:T // 128, :], xap, writes=[xsB[b]])
                if xm_ is not None:
                    P.op("pool", lambda e, b=b, T=T, xm_=xm_: e.tensor_scalar(
                        out=xs[b][:, 0:T // 128, :], in0=xs[b][:, 0:T // 128, :], scalar1=rk[:, xm_:xm_ + 1], scalar2=None, op0=ALU.mult),
                        reads=[xsB[b], rkB], writes=[xsB[b]])

            load_grp(0)
            k = 0
            for gi, (t0, T) in enumerate(groups1):
                if gi + 1 < len(groups1):
                    load_grp(gi + 1)
                b = gi % 2
                for s in range(T // 128):
                    for half in range(2):
                        pi = k % 4
                        k += 1
                        for c in range(RC):
                            P.op("pe", lambda e, pi=pi, c=c, s=s, half=half, b=b: e.matmul(
                                py[pi][:, :], lhsT=onT[b][:, c, s * 128:(s + 1) * 128], rhs=wo[:, c, half * 512:(half + 1) * 512],
                                start=(c == 0), stop=(c == RC - 1)),
                                reads=[onTB[b], woB], writes=[pyB[pi]], mark=(c == RC - 1))
                        P.op("dve", lambda e, pi=pi, s=s, half=half, b=b: e.tensor_tensor(
                            out=xm[b][:, s, half * 512:(half + 1) * 512], in0=py[pi][:, :], in1=xs[b][:, s, half * 512:(half + 1) * 512], op=ALU.add),
                            reads=[pyB[pi], xsB[b]], writes=[xmsB[b]])
                xmB[t0] = C.buf()
                P.dma("pool", xm_d[t0:t0 + T, :].rearrange("(s p) d -> p s d", p=128), xm[b][:, 0:T // 128, :],
                      reads=[xmsB[b]], writes=[xmB[t0]])
            P.barrier()
            P.flush()

        with ExitStack() as es:
            gcol = C.sb(es, "gcol_sb", [128, 8], F32)
            gB = C.buf()
            P.dma("sp", gcol[:, :], g_d, writes=[gB])
            cw = C.sb(es, "cw_sb", [128, 2 * NFC, 3], F32)
            cbs = C.sb(es, "cb_sb", [128, 2 * NFC], F32)
            cwB = C.buf()
            P.dma("sp", cw[:, :, :], cw_d, writes=[cwB])
            P.dma("sp", cbs[:, :], cb_d, writes=[cwB])
            win = C.sb(es, "win_sb", [128, 8, 2 * DFF], BF16)
            wout = C.sb(es, "wout_sb", [128, NFC, D], BF16)
            winB, woutB = C.buf(), C.buf()
            with ExitStack() as es_st:
                load_weight_bf16(C, es_st, win_d, win, winB, 8, 2 * DFF, gcol, gB, colblk=DFF // 2, tag="win")
                load_weight_bf16(C, es_st, wout_d, wout, woutB, NFC, D, tag="wout")
                P.barrier()
                P.flush()
            winB.frozen = True
            woutB.frozen = True
            cwB.frozen = True
            if final:
                gf = C.sb(es, "gf_sb", [128, D], F32)
                gfB = C.buf()
                P.dma("sp", gf[:, :], gf_d, writes=[gfB])
                gfB.frozen = True
            TT = 256
            groups2 = [(0, 128)] + [(128 + i * TT, TT) for i in range(OWN // TT)]
            nrm = NormT(C, es, "nB")
            xmt = [C.sb(es, "xmt%d" % i, [128, 2, D], F32) for i in range(2)]
            xmtB = [C.buf() for _ in range(2)]
            hT = C.sb(es, "hTB", [128, 8, TT], BF16)
            hTB = C.buf()
            aT = C.sb(es, "aT", [128, NFC, TT], BF16)
            aTB = C.buf()
            NB2 = 2
            NBS = 2 if final else 3
            us = [[C.sb(es, "us%d_%d" % (w_, i), [128, TT + 2], F32) for i in range(NBS)] for w_ in range(2)]
            usB = [[C.buf() for _ in range(NBS)] for _ in range(2)]
            tc_ = [[C.sb(es, "tc%d_%d" % (w_, i), [128, TT], F32) for i in range(NBS)] for w_ in range(2)]
            tcB = [[C.buf() for _ in range(NBS)] for _ in range(2)]
            sg = [C.sb(es, "sg%d" % i, [128, TT], F32) for i in range(NBS)]
            sgB = [C.buf() for _ in range(NBS)]
            stash = C.sb(es, "stash", [128, 2 * NFC, 2], F32)
            stashB = [C.buf() for _ in range(2 * NFC)]
            xo = xmt
            xoB = xmtB
            P.op("pool", lambda e: e.memset(stash[:, :, :], 0.0), writes=stashB)
            pT = C.ps(es, "pTB", [128, 8, 128], BF16)
            pTB = C.buf()
            pu = [[C.ps(es, "pu%d_%d" % (w_, i), [128, TT], F32) for i in range(NB2)] for w_ in range(2)]
            puB = [[C.buf() for _ in range(NB2)] for _ in range(2)]
            py2 = [C.ps(es, "py2_%d" % i, [128, 512], F32) for i in range(2)]
            py2B = [C.buf() for _ in range(2)]
            if final:
                nf = NormT(C, es, "nF")
                fin = [C.sb(es, "fin%d" % i, [128, D], F32) for i in range(2)]
                finB = [C.buf() for _ in range(2)]

            def load_grp2(gi):
                t0, T = groups2[gi]
                b = gi % 2
                rd = [xmB[t] for t in xmB if t < t0 + T and t + (128 if t == 0 else 512) > t0]
                P.dma("sp", xmt[b][:, 0:T // 128, :], xm_d[t0:t0 + T, :].rearrange("(s p) d -> p s d", p=128),
                      reads=rd, writes=[xmtB[b]])

            load_grp2(0)
            kk = 0
            fk = 0
            pend = None
            for gi, (t0, T) in enumerate(groups2):
                if gi + 1 < len(groups2):
                    load_grp2(gi + 1)
                b = gi % 2
                ns = T // 128
                for s in range(ns):
                    si = nrm.rstd(xmt[b][:, s, :], xmtB[b])
                    nrm.normalize(si, [(xmt[b][:, s, :], D)], xmtB[b])
                    nrm.transpose_to(si, pT, pTB, lambda s=s: hT[:, :, s * 128:(s + 1) * 128], hTB)
                for i in range(NFC):
                    ub = kk % NB2
                    sb_ = kk % NBS
                    kk += 1
                    for w_ in range(2):
                        ch = i + NFC * w_
                        for c in range(8):
                            P.op("pe", lambda e, w_=w_, ub=ub, c=c, ch=ch, T=T: e.matmul(
                                pu[w_][ub][:, 0:T], lhsT=win[:, c, ch * 128:(ch + 1) * 128], rhs=hT[:, c, 0:T],
                                start=(c == 0), stop=(c == 7)),
                                reads=[winB, hTB], writes=[puB[w_][ub]], mark=(c == 7))
                        if gi == 0:
                            P.op("act", lambda e, w_=w_, ub=ub, ch=ch, T=T: e.activation(
                                out=stash[:, ch, :], in_=pu[w_][ub][:, T - 2:T], func=AF.Copy),
                                reads=[puB[w_][ub]], writes=[stashB[ch]])
                            continue
                        u_ = us[w_][sb_]
                        P.op("pool", lambda e, u_=u_, ch=ch: e.tensor_copy(out=u_[:, 0:2], in_=stash[:, ch, :]),
                             reads=[stashB[ch]], writes=[usB[w_][sb_]])
                        P.op("act", lambda e, u_=u_, w_=w_, ub=ub, T=T: e.activation(out=u_[:, 2:2 + T], in_=pu[w_][ub][:, 0:T], func=AF.Copy),
                             reads=[puB[w_][ub]], writes=[usB[w_][sb_]])
                        P.op("pool", lambda e, u_=u_, ch=ch, T=T: e.tensor_copy(out=stash[:, ch, :], in_=u_[:, T:T + 2]),
                             reads=[usB[w_][sb_]], writes=[stashB[ch]])
                        t_ = tc_[w_][sb_]
                        P.op("act", lambda e, w_=w_, ub=ub, t_=t_, ch=ch, T=T: e.activation(
                            out=t_[:, 0:T], in_=pu[w_][ub][:, 0:T], func=AF.Identity, scale=cw[:, ch, 2:3], bias=cbs[:, ch:ch + 1]),
                            reads=[puB[w_][ub], cwB], writes=[tcB[w_][sb_]])
                        for j in (1, 0):
                            P.op("dve", lambda e, u_=u_, t_=t_, ch=ch, T=T, j=j: e.scalar_tensor_tensor(
                                out=t_[:, 0:T], in0=u_[:, j:j + T], scalar=cw[:, ch, j:j + 1], in1=t_[:, 0:T], op0=ALU.mult, op1=ALU.add),
                                reads=[usB[w_][sb_], cwB, tcB[w_][sb_]], writes=[tcB[w_][sb_]])
                    if gi == 0:
                        continue

                    def gate(sb_, i, T):
                        P.op("act", lambda e: e.activation(out=sg[sb_][:, 0:T], in_=tc_[0][sb_][:, 0:T], func=AF.Silu),
                             reads=[tcB[0][sb_]], writes=[sgB[sb_]])
                        P.op("dve", lambda e: e.tensor_tensor(out=aT[:, i, 0:T], in0=sg[sb_][:, 0:T], in1=tc_[1][sb_][:, 0:T], op=ALU.mult),
                             reads=[sgB[sb_], tcB[1][sb_]], writes=[aTB])

                    if pend is not None:
                        gate(*pend)
                    pend = (sb_, i, T)
                if pend is not None:
                    gate(*pend)
                    pend = None
                if gi == 0:
                    continue
                for s in range(ns):
                    for half in range(2):
                        for i in range(NFC):
                            P.op("pe", lambda e, half=half, i=i, s=s: e.matmul(
                                py2[half][:, :], lhsT=aT[:, i, s * 128:(s + 1) * 128], rhs=wout[:, i, half * 512:(half + 1) * 512],
                                start=(i == 0), stop=(i == NFC - 1)),
                                reads=[aTB, woutB], writes=[py2B[half]], mark=(i == NFC - 1))
                        P.op("dve", lambda e, half=half, s=s, b=b: e.tensor_tensor(
                            out=xo[b][:, s, half * 512:(half + 1) * 512], in0=py2[half][:, :], in1=xmt[b][:, s, half * 512:(half + 1) * 512], op=ALU.add),
                            reads=[py2B[half], xmtB[b]], writes=[xoB[b]])
                    if final:
                        si = nf.rstd(xo[b][:, s, :], xoB[b])
                        fb = fk % 2
                        fk += 1
                        P.op("act", lambda e, fb=fb, b=b, s=s, si=si: e.activation(
                            out=fin[fb][:, :], in_=xo[b][:, s, :], func=AF.Copy, scale=nf.st[si][:, 1:2]),
                            reads=[xoB[b], nf.stB[si]], writes=[finB[fb]])
                        P.op("pool", lambda e, fb=fb: e.tensor_tensor(out=fin[fb][:, :], in0=fin[fb][:, :], in1=gf[:, :], op=ALU.mult),
                             reads=[finB[fb], gfB], writes=[finB[fb]])
                        r0 = t0 - 128 + s * 128
                        P.dma("sp", out_d[r0:r0 + 128, :], fin[fb][:, :], reads=[finB[fb]], writes=[C.buf()])
                if not final:
                    r0 = t0 - 128
                    P.dma("sp", out_d[r0:r0 + T, :].rearrange("(s p) d -> p s d", p=128), xo[b][:, 0:ns, :],
                          reads=[xoB[b]], writes=[C.buf()])
            P.barrier()
            P.flush()


def inputs_B(inp, layer, onT_b, xres_b, hh, consts):
    t0 = hh * OWN
    R = onT_b.shape[0]
    on = np.zeros((R, NTOK), onT_b.dtype)
    xx = np.zeros((NTOK, D), np.float32)
    if t0 > 0:
        on[:, 0:HALO] = onT_b[:, t0 - HALO:t0]
        xx[0:HALO] = xres_b[t0 - HALO:t0]
    on[:, HALO:] = onT_b[:, t0:t0 + OWN]
    xx[HALO:] = xres_b[t0:t0 + OWN]
    wo = np.asarray(inp["w_fox_out"][0] if layer == 0 else inp["w_mla_out"][0], np.float32)
    cw = np.asarray(inp["ffn_conv_w"][layer], np.float32)
    d = {
        "onT": on, "x": xx, "wo": np.ascontiguousarray(wo),
        "gcol": col128(inp["ffn_norm"][layer]),
        "w_in": np.ascontiguousarray(np.asarray(inp["w_ffn_in"][layer], np.float32)),
        "cw": np.ascontiguousarray(cw.T.reshape(2 * NFC, 128, 3).transpose(1, 0, 2)),
        "cb": col128(inp["ffn_conv_b"][layer]),
        "w_out": np.ascontiguousarray(np.asarray(inp["w_ffn_out"][layer], np.float32)),
    }
    if layer == 1:
        d["gfin"] = np.ascontiguousarray(np.broadcast_to(np.asarray(inp["final_norm"], np.float32)[None, :], (128, D)))
    d.update(consts)
    return d


MLA_SCALE = 192.0 ** -0.5


def build_C():
    nc = bass.Bass("TRN2", target_bir_lowering=False)
    with ExitStack() as es0:
        C = Ctx(nc, es0)
        io = dict(x=C.dram_in("x", [S, D], F32), wdkv=C.dram_in("wdkv", [D, 320], F32), gkv=C.dram_in("gkv", [128, 8], F32),
                  kvn=C.dram_in("kvn", [128, 2], F32), wuk=C.dram_in("wuk", [256, 1024], F32), wuv=C.dram_in("wuv", [256, 1024], F32),
                  gmla=C.dram_in("gmla", [128, 8], F32), wdq=C.dram_in("wdq", [D, 768], F32), qn=C.dram_in("qn", [128, 6], F32),
                  wuqn=C.dram_in("wuqn", [768, 1024], F32), wuqr=C.dram_in("wuqr", [768, 512], F32),
                  cos2=C.dram_in("cos2", [64, S], F32), sin2=C.dram_in("sin2", [64, S], F32),
                  out=C.dram_out("onT", [1024, S], BF16))
        load_consts(C, es0)
        emit_C(C, io)
    return nc


def emit_C(C, io):
    P = C.P
    C.pfx = "C_"
    with ExitStack() as es0:
        (x_d, wdkv_d, gkv_d, kvn_d, wuk_d, wuv_d, gmla_d, wdq_d, qn_d, wuqn_d, wuqr_d, cos_d, sin_d, out_d) = (
            io[k_] for k_ in ("x", "wdkv", "gkv", "kvn", "wuk", "wuv", "gmla", "wdq", "qn", "wuqn", "wuqr", "cos2", "sin2", "out"))
        QN = C.dram_scr("QN", [8, 128, S], BF16)
        QR = C.dram_scr("QR", [8, 64, S], BF16)
        KN = C.dram_scr("KN", [8, 128, S], BF16)
        KR = C.dram_scr("KR", [64, S], BF16)
        VS = C.dram_scr("VS", [8, 128, NT, 128], BF16)
        QNB = [[C.buf() for _ in range(NG)] for _ in range(8)]
        QRB = [[C.buf() for _ in range(NG)] for _ in range(8)]
        KNB = [[C.buf() for _ in range(NG)] for _ in range(8)]
        VSB = [[C.buf() for _ in range(NG)] for _ in range(8)]
        KRB = [C.buf() for _ in range(NG)]

        with ExitStack() as es:
            small = {}
            for nm, d_, n_ in (("gkv", gkv_d, 8), ("kvn", kvn_d, 2), ("gmla", gmla_d, 8), ("qn", qn_d, 6)):
                t_ = C.sb(es, nm + "_sb", [128, n_], F32)
                b_ = C.buf()
                P.dma("sp", t_[:, :], d_, writes=[b_])
                small[nm] = (t_, b_)
            wdkv = C.sb(es, "wdkv_sb", [128, 8, 320], BF16)
            wkrA = C.sb(es, "wkrA_sb", [128, 8, 128], BF16)
            wkrB = C.sb(es, "wkrB_sb", [128, 8, 128], BF16)
            wuk = C.sb(es, "wuk_sb", [128, 2, 1024], BF16)
            wuv = C.sb(es, "wuv_sb", [128, 2, 1024], BF16)
            wdq = C.sb(es, "wdq_sb", [128, 8, 768], BF16)
            wuqn = C.sb(es, "wuqn_sb", [128, 6, 1024], BF16)
            wuqr = C.sb(es, "wuqr_sb", [128, 6, 8, 64], BF16)
            wuqA = C.sb(es, "wuqA_sb", [128, 6, 8, 128], BF16)
            wuqB = C.sb(es, "wuqB_sb", [128, 6, 8, 128], BF16)
            wB = {k_: C.buf() for k_ in ("dkv", "dkvs", "uk", "uv", "dq", "uqn", "uqr", "uqs")}
            with ExitStack() as es_st:
                load_weight_bf16(C, es_st, wdkv_d, wdkv, wB["dkv"], 8, 320, *small["gkv"], tag="wdkv")
                load_weight_bf16(C, es_st, wuk_d, wuk, wB["uk"], 2, 1024, *small["kvn"], tag="wuk")
                load_weight_bf16(C, es_st, wuv_d, wuv, wB["uv"], 2, 1024, *small["kvn"], tag="wuv")
                load_weight_bf16(C, es_st, wdq_d, wdq, wB["dq"], 8, 768, *small["gmla"], tag="wdq")
                load_weight_bf16(C, es_st, wuqn_d, wuqn, wB["uqn"], 6, 1024, *small["qn"], tag="wuqn")
                load_weight_bf16(C, es_st, wuqr_d, wuqr[:, :, :, :].rearrange("p l h d -> p l (h d)"), wB["uqr"], 6, 512, *small["qn"], tag="wuqr")
                def neg(dst, src, rd, wr):
                    P.op("dve", lambda e: e.tensor_scalar(out=dst, in0=src, scalar1=-1.0, scalar2=None, op0=ALU.mult), reads=[rd], writes=[wr])

                def cpy(dst, src, rd, wr):
                    P.op("dve", lambda e: e.tensor_copy(out=dst, in_=src), reads=[rd], writes=[wr])

                cpy(wkrA[:, :, 0:64], wdkv[:, :, 256:320], wB["dkv"], wB["dkvs"])
                neg(wkrA[:, :, 64:96], wdkv[:, :, 288:320], wB["dkv"], wB["dkvs"])
                cpy(wkrA[:, :, 96:128], wdkv[:, :, 256:288], wB["dkv"], wB["dkvs"])
                neg(wkrB[:, :, 0:32], wdkv[:, :, 288:320], wB["dkv"], wB["dkvs"])
                cpy(wkrB[:, :, 32:64], wdkv[:, :, 256:288], wB["dkv"], wB["dkvs"])
                cpy(wkrB[:, :, 64:128], wdkv[:, :, 256:320], wB["dkv"], wB["dkvs"])
                for l in range(6):
                    cpy(wuqA[:, l, :, 0:64], wuqr[:, l, :, :], wB["uqr"], wB["uqs"])
                    neg(wuqA[:, l, :, 64:96], wuqr[:, l, :, 32:64], wB["uqr"], wB["uqs"])
                    cpy(wuqA[:, l, :, 96:128], wuqr[:, l, :, 0:32], wB["uqr"], wB["uqs"])
                    neg(wuqB[:, l, :, 0:32], wuqr[:, l, :, 32:64], wB["uqr"], wB["uqs"])
                    cpy(wuqB[:, l, :, 32:64], wuqr[:, l, :, 0:32], wB["uqr"], wB["uqs"])
                    cpy(wuqB[:, l, :, 64:128], wuqr[:, l, :, :], wB["uqr"], wB["uqs"])
                if io.get("pre_main") is not None:
                    io["pre_main"]()
                P.barrier()
                P.flush()
            for b_ in wB.values():
                b_.frozen = True
            nrm = NormT(C, es, "nC")
            nL = NormT(C, es, "nL", width=256)
            nQ = NormT(C, es, "nQ", width=768)
            xt = [C.sb(es, "xtC%d" % i, [128, D], F32) for i in range(3)]
            xtB = [C.buf() for _ in range(3)]
            hT = [C.sb(es, "hTC%d" % i, [128, 8, 512], BF16) for i in range(2)]
            hTB = [C.buf() for _ in range(2)]
            latT = C.sb(es, "latT", [128, 2, 512], BF16)
            latTB = C.buf()
            cqT = C.sb(es, "cqT", [128, 6, 512], BF16)
            cqTB = C.buf()
            KNst = C.sb(es, "KNst", [128, 8, 512], BF16)
            QNst = C.sb(es, "QNst", [128, 8, 512], BF16)
            QRst = C.sb(es, "QRst", [64, 8, 512], BF16)
            Vst = C.sb(es, "VstC", [128, 8, 4, 128], BF16)
            KRst = C.sb(es, "KRst", [64, 512], BF16)
            KNstB, QNstB, QRstB, VstB, KRstB = C.buf(), C.buf(), C.buf(), C.buf(), C.buf()
            cs = [[C.sb(es, "cs%d_%d" % (j, i), [64, 512], F32) for i in range(2)] for j in range(2)]
            csB = [C.buf() for _ in range(2)]
            t1 = [C.sb(es, "ropet1_%d" % i, [64, 512], F32) for i in range(2)]
            t2 = [C.sb(es, "ropet2_%d" % i, [64, 512], F32) for i in range(2)]
            t1B = [C.buf() for _ in range(2)]
            t2B = [C.buf() for _ in range(2)]
            pT = C.ps(es, "pTC", [128, 8, 128], BF16)
            pTB = C.buf()
            pp = [C.ps(es, "ppC%d" % i, [128, 512], F32) for i in range(6)]
            ppB = [C.buf() for _ in range(6)]
            prr = [0]

            def nextp():
                i = prr[0] % 6
                prr[0] += 1
                return pp[i], ppB[i]

            rk = [0]

            def rope(pa, paB, pb_, pbB, cb_, dst_ap, dstB):
                i = rk[0] % 2
                rk[0] += 1
                P.op("dve", lambda e, i=i: e.tensor_tensor(out=t1[i][:, :], in0=pa[0:64, :], in1=cs[0][cb_][:, :], op=ALU.mult),
                     reads=[paB, csB[cb_]], writes=[t1B[i]])
                P.op("dve", lambda e, i=i: e.tensor_tensor(out=t2[i][:, :], in0=pb_[0:64, :], in1=cs[1][cb_][:, :], op=ALU.mult),
                     reads=[pbB, csB[cb_]], writes=[t2B[i]])
                P.op("pool", lambda e, i=i: e.tensor_tensor(out=dst_ap, in0=t1[i][:, :], in1=t2[i][:, :], op=ALU.add),
                     reads=[t1B[i], t2B[i]], writes=[dstB])

            xv = x_d.rearrange("(n p) d -> n p d", p=128)

            def load_x(ti):
                i = ti % 3
                P.dma("sp", xt[i][:, :], xv[ti], writes=[xtB[i]])

            def load_cs(tg):
                b = tg % 2
                P.dma("sp", cs[0][b][:, :], cos_d[:, tg * 512:(tg + 1) * 512], writes=[csB[b]])
                P.dma("sp", cs[1][b][:, :], sin_d[:, tg * 512:(tg + 1) * 512], writes=[csB[b]])

            load_x(0)
            load_x(1)
            load_cs(0)
            ev = [0]

            def evac(out_ap, in_ap, inB, outB):
                ev[0] += 1
                if ev[0] % 2:
                    P.op("act", lambda e: e.activation(out=out_ap, in_=in_ap, func=AF.Copy), reads=[inB], writes=[outB])
                else:
                    P.op("dve", lambda e: e.tensor_copy(out=out_ap, in_=in_ap), reads=[inB], writes=[outB])

            import os
            PARTS = os.environ.get('C1_PARTS', 'lat,kr,kn,v,q,qh').split(',')
            for tg in range(int(os.environ.get('C1_NG', NG))):
                hb = tg % 2
                cb_ = tg % 2
                if tg + 1 < NG:
                    load_cs(tg + 1)
                for s in range(4):
                    ti = tg * 4 + s
                    if ti + 2 < NT:
                        load_x(ti + 2)
                    xi = ti % 3
                    si = nrm.rstd(xt[xi][:, :], xtB[xi])
                    nrm.normalize(si, [(xt[xi][:, :], D)], xtB[xi])
                    nrm.transpose_to(si, pT, pTB, lambda hb=hb, s=s: hT[hb][:, :, s * 128:(s + 1) * 128], hTB[hb])
                h_ = hT[hb]
                hB_ = hTB[hb]
                for s in (range(4) if 'lat' in PARTS else []):
                    pc, pcB = nextp()
                    for c in range(8):
                        P.op("pe", lambda e, pc=pc, c=c, s=s, h_=h_: e.matmul(pc[:, 0:256], lhsT=h_[:, c, s * 128:(s + 1) * 128], rhs=wdkv[:, c, 0:256],
                                                                      start=(c == 0), stop=(c == 7)),
                             reads=[hB_, wB["dkv"]], writes=[pcB], mark=(c == 7))
                    si = nL.rstd(None, pcB, parts=[(pc[:, 0:256], 256)])
                    nL.normalize(si, [(pc[:, 0:256], 256)], pcB)
                    nL.transpose_to(si, pT, pTB, lambda s=s: latT[:, :, s * 128:(s + 1) * 128], latTB)
                pk, pkB = nextp()
                pks, pksB = nextp()
                for (dst, dB, w_ap) in () if 'kr' not in PARTS else ((pk, pkB, lambda c: wkrA[:, c, :]), (pks, pksB, lambda c: wkrB[:, c, :])):
                    for c in range(8):
                        P.op("pe", lambda e, dst=dst, c=c, w_ap=w_ap, h_=h_: e.matmul(dst[:, :], lhsT=w_ap(c), rhs=h_[:, c, :], start=(c == 0), stop=(c == 7)),
                             reads=[hB_, wB["dkv"], wB["dkvs"]], writes=[dB], mark=(c == 7))
                if 'kr' in PARTS:
                    rope(pk, pkB, pks, pksB, cb_, KRst[:, :], KRstB)
                    P.dma("pool", KR[:, tg * 512:(tg + 1) * 512], KRst[:, :], reads=[KRstB], writes=[KRB[tg]])
                for h in (range(8) if 'kn' in PARTS else []):
                    pn, pnB = nextp()
                    for l in range(2):
                        P.op("pe", lambda e, pn=pn, l=l, h=h: e.matmul(pn[:, :], lhsT=wuk[:, l, h * 128:(h + 1) * 128], rhs=latT[:, l, :], start=(l == 0), stop=(l == 1)),
                             reads=[latTB, wB["uk"]], writes=[pnB], mark=(l == 1))
                    evac(KNst[:, h, :], pn[:, :], pnB, KNstB)
                if 'kn' in PARTS:
                    P.dma("pool", KN[:, :, tg * 512:(tg + 1) * 512].rearrange("h p t -> p h t"), KNst[:, :, :], reads=[KNstB],
                          writes=[KNB[h_i][tg] for h_i in range(8)])
                for s in (range(4) if 'v' in PARTS else []):
                    for half in range(2):
                        pv_, pvB_ = nextp()
                        for l in range(2):
                            P.op("pe", lambda e, pv_=pv_, l=l, s=s, half=half: e.matmul(
                                pv_[:, :], lhsT=latT[:, l, s * 128:(s + 1) * 128], rhs=wuv[:, l, half * 512:(half + 1) * 512], start=(l == 0), stop=(l == 1)),
                                reads=[latTB, wB["uv"]], writes=[pvB_], mark=(l == 1))
                        evac(Vst[:, half * 4:(half + 1) * 4, s, :], pv_[:, :].rearrange("p (h d) -> p h d", d=128), pvB_, VstB)
                for h in (range(8) if 'v' in PARTS else []):
                    P.dma("pool", VS[h, :, tg * 4:(tg + 1) * 4, :], Vst[:, h, :, :], reads=[VstB], writes=[VSB[h][tg]])
                for s in (range(4) if 'q' in PARTS else []):
                    pa, paB = nextp()
                    pb2, pb2B = nextp()
                    for (dst, dB, c0, c1) in ((pa, paB, 0, 512), (pb2, pb2B, 512, 768)):
                        for c in range(8):
                            P.op("pe", lambda e, dst=dst, c=c, s=s, c0=c0, c1=c1, h_=h_: e.matmul(
                                dst[:, 0:c1 - c0], lhsT=h_[:, c, s * 128:(s + 1) * 128], rhs=wdq[:, c, c0:c1], start=(c == 0), stop=(c == 7)),
                                reads=[hB_, wB["dq"]], writes=[dB], mark=(c == 7))
                    jb = C.buf()
                    si = nQ.rstd2([(pa[:, 0:512], 512, paB), (pb2[:, 0:256], 256, pb2B)])
                    nQ.normalize2(si, [(pa[:, 0:512], 512, paB), (pb2[:, 0:256], 256, pb2B)])
                    nQ.transpose_to(si, pT, pTB, lambda s=s: cqT[:, :, s * 128:(s + 1) * 128], cqTB)
                for h in (range(8) if 'qh' in PARTS else []):
                    pn, pnB = nextp()
                    for l in range(6):
                        P.op("pe", lambda e, pn=pn, l=l, h=h: e.matmul(pn[:, :], lhsT=wuqn[:, l, h * 128:(h + 1) * 128], rhs=cqT[:, l, :], start=(l == 0), stop=(l == 5)),
                             reads=[cqTB, wB["uqn"]], writes=[pnB], mark=(l == 5))
                    evac(QNst[:, h, :], pn[:, :], pnB, QNstB)
                    pr, prB = nextp()
                    prs, prsB = nextp()
                    for (dst, dB, w_t) in ((pr, prB, wuqA), (prs, prsB, wuqB)):
                        for l in range(6):
                            P.op("pe", lambda e, dst=dst, l=l, h=h, w_t=w_t: e.matmul(dst[:, :], lhsT=w_t[:, l, h, :], rhs=cqT[:, l, :], start=(l == 0), stop=(l == 5)),
                                 reads=[cqTB, wB["uqr"], wB["uqs"]], writes=[dB], mark=(l == 5))
                    rope(pr, prB, prs, prsB, cb_, QRst[:, h, :], QRstB)
                if 'qh' in PARTS:
                    P.dma("pool", QN[:, :, tg * 512:(tg + 1) * 512].rearrange("h p t -> p h t"), QNst[:, :, :], reads=[QNstB],
                          writes=[QNB[h_i][tg] for h_i in range(8)])
                    P.dma("pool", QR[:, :, tg * 512:(tg + 1) * 512].rearrange("h p t -> p h t"), QRst[:, :, :], reads=[QRstB],
                          writes=[QRB[h_i][tg] for h_i in range(8)])
            P.barrier()
            P.flush()

        with ExitStack() as es:
            QNh = [C.sb(es, "QNh%d" % i, [128, S], BF16) for i in range(2)]
            QRh = [C.sb(es, "QRh%d" % i, [128, S], BF16) for i in range(2)]
            KNh = [C.sb(es, "KNh%d" % i, [128, S], BF16) for i in range(2)]
            Vh = [C.sb(es, "VhC%d" % i, [128, NT, 128], BF16) for i in range(2)]
            KRs = C.sb(es, "KRs", [128, S], BF16)
            QNhB = [C.buf() for _ in range(2)]
            QRhB = [C.buf() for _ in range(2)]
            KNhB = [C.buf() for _ in range(2)]
            VhB = [C.buf() for _ in range(2)]
            KRsB = C.buf()
            P.op("pool", lambda e: e.memset(KRs[64:128, :], 0.0), writes=[KRsB])
            for i_ in range(2):
                P.op("pool", lambda e, i_=i_: e.memset(QRh[i_][64:128, :], 0.0), writes=[QRhB[i_]])
            P.dma("sp", KRs[0:64, :], KR, reads=KRB, writes=[KRsB])
            KRsB.frozen = True

            def load_head(h):
                hb = h % 2
                P.dma("sp", QNh[hb][:, :], QN[h], reads=QNB[h], writes=[QNhB[hb]])
                P.dma("sp", QRh[hb][0:64, :], QR[h], reads=QRB[h], writes=[QRhB[hb]])
                P.dma("sp", KNh[hb][:, :], KN[h], reads=KNB[h], writes=[KNhB[hb]])
                P.dma("sp", Vh[hb][:, :, :], VS[h], reads=VSB[h], writes=[VhB[hb]])

            def kparts(h):
                hb = h % 2
                parts = [(lambda kt, hb=hb: KNh[hb][:, kt * 128:(kt + 1) * 128], lambda a, b, hb=hb: QNh[hb][:, a:b]),
                         (lambda kt: KRs[:, kt * 128:(kt + 1) * 128], lambda a, b, hb=hb: QRh[hb][:, a:b])]
                return parts, [KNhB[hb], QNhB[hb], QRhB[hb], KRsB], (lambda kt, hb=hb: Vh[hb][:, kt, :]), VhB[hb]

            import os
            if not os.environ.get("SKIP_C2"):
                attention_phase(C, es, 8, 128, kparts, load_head, out_d, MLA_SCALE, C.mask_mla, lambda h, kt: None, "sep", after_head=io.get("after_head"))
            P.barrier()
            P.flush()


def _rstd2(self, parts):
    P = self.C.P
    i = self.k % 2
    self.k += 1
    st, stB = self.st[i], self.stB[i]
    off = 0
    for j, (ap, w, B_) in enumerate(parts):
        P.op("act", lambda e, ap=ap, w=w, off=off, j=j: e.activation(
            out=self.junk[:, off:off + w], in_=ap, func=AF.Square, accum_out=st[:, 2 + j:3 + j]),
            reads=[B_], writes=[self.junkB, stB])
        off += w
    if len(parts) == 2:
        P.op("dve", lambda e: e.tensor_tensor(out=st[:, 2:3], in0=st[:, 2:3], in1=st[:, 3:4], op=ALU.add), reads=[stB], writes=[stB])
    P.op("act", lambda e: e.activation(out=st[:, 0:1], in_=st[:, 2:3], func=AF.Sqrt, scale=1.0 / self.width, bias=EPS), reads=[stB], writes=[stB])
    P.op("dve", lambda e: e.reciprocal(out=st[:, 1:2], in_=st[:, 0:1]), reads=[stB], writes=[stB])
    return i


def _normalize2(self, i, parts):
    P = self.C.P
    st, stB = self.st[i], self.stB[i]
    off = 0
    for (ap, w, B_) in parts:
        P.op("act", lambda e, ap=ap, w=w, off=off: e.activation(out=self.xn[i][:, off:off + w], in_=ap, func=AF.Copy, scale=st[:, 1:2]),
             reads=[B_, stB], writes=[self.xnB[i]])
        off += w


NormT.rstd2 = _rstd2
NormT.normalize2 = _normalize2


def rope_tables():
    pos = np.arange(S, dtype=np.float32)
    inv = (np.float32(10000.0) ** (-np.arange(0, 64, 2, dtype=np.float32) / np.float32(64))).astype(np.float32)
    ang = pos[:, None] * inv[None, :]
    cos = np.cos(ang).astype(np.float32).T
    sin = np.sin(ang).astype(np.float32).T
    return np.ascontiguousarray(np.concatenate([cos, cos], 0)), np.ascontiguousarray(np.concatenate([sin, sin], 0))


def inputs_C(inp, x1_b, hh, consts, tabs):
    wuq = np.asarray(inp["w_uq"][0], np.float32).reshape(768, 16, 192)[:, hh * 8:(hh + 1) * 8, :]
    d = {
        "x": np.ascontiguousarray(x1_b, dtype=np.float32),
        "wdkv": np.ascontiguousarray(np.asarray(inp["w_dkv"], np.float32)),
        "gkv": col128(inp["kv_in_norm"]),
        "kvn": col128(inp["kv_norm"]),
        "wuk": np.ascontiguousarray(np.asarray(inp["w_uk"], np.float32)[:, hh * 1024:(hh + 1) * 1024]),
        "wuv": np.ascontiguousarray(np.asarray(inp["w_uv"], np.float32)[:, hh * 1024:(hh + 1) * 1024]),
        "gmla": col128(inp["mla_norm"][0]),
        "wdq": np.ascontiguousarray(np.asarray(inp["w_dq"][0], np.float32)),
        "qn": col128(inp["q_norm"][0]),
        "wuqn": np.ascontiguousarray(wuq[:, :, 0:128].reshape(768, 1024)),
        "wuqr": np.ascontiguousarray(wuq[:, :, 128:192].reshape(768, 512)),
        "cos2": tabs[0], "sin2": tabs[1],
    }
    d.update(consts)
    return d


def exchange(C, groups, src, dst_fn, nchunk, rows, slots, standalone=True):
    P = C.P
    snd, rcv, sndB, rcvB = slots
    ns = len(snd)
    if standalone:
        P.barrier()

    def put(i):
        k = i % ns
        P.dma("sp", snd[k], src[i * rows:(i + 1) * rows, :], writes=[sndB[k]])

    for i in range(min(ns, nchunk)):
        put(i)
    for i in range(nchunk):
        k = i % ns
        P.op("pool", lambda e, k=k: e.collective_compute("AllGather", ALU.bypass, replica_groups=groups, ins=[snd[k]], outs=[rcv[k]]),
             reads=[sndB[k]], writes=[rcvB[k]])
        for r in range(2):
            P.dma("sp", dst_fn(r, i), rcv[k][r * rows:(r + 1) * rows, :], reads=[rcvB[k]], writes=[C.buf()])
        if i + ns < nchunk:
            put(i + ns)
    if standalone:
        P.barrier()
        P.flush()


def make_head_exchange(C, groups, snd_all, dst_fn, heads_per_chunk, slots):
    P = C.P
    snd, rcv, sndB, rcvB = slots
    ns = len(snd)

    def hook(h, outB):
        if (h + 1) % heads_per_chunk:
            return
        i = h // heads_per_chunk
        k = i % ns
        rd = [b for hh in range(h + 1 - heads_per_chunk, h + 1) for b in outB[hh]]
        P.dma("pool", snd[k], snd_all[i * 128:(i + 1) * 128, :], reads=rd, writes=[sndB[k]])
        P.op("pool", lambda e, k=k: e.collective_compute("AllGather", ALU.bypass, replica_groups=groups, ins=[snd[k]], outs=[rcv[k]]),
             reads=[sndB[k]], writes=[rcvB[k]])
        for r in range(2):
            P.dma("pool", dst_fn(r, i), rcv[k][r * 128:(r + 1) * 128, :], reads=[rcvB[k]], writes=[C.buf()])
    return hook


def build_fused(n_cores=8):
    nc = bass.Bass("TRN2", target_bir_lowering=False)
    groups = [[2 * i, 2 * i + 1] for i in range(n_cores // 2)]
    with ExitStack() as es0:
        C = Ctx(nc, es0)
        P = C.P
        load_consts(C, es0)
        rk_d = C.dram_in("rk", [128, 4], F32)
        rk = C.sb(es0, "rk_sb", [128, 4], F32)
        rkB = C.buf()
        P.dma("sp", rk[:, :], rk_d, writes=[rkB])
        rkB.frozen = True
        tok = lambda ap: ap.rearrange("(s p) d -> p s d", p=128)
        NSLOT = 2
        sl16 = ([C.dram_scr("cc_s16_%d" % i, [128, S], BF16) for i in range(NSLOT)], [C.dram_scr("cc_r16_%d" % i, [256, S], BF16) for i in range(NSLOT)],
                [C.buf() for _ in range(NSLOT)], [C.buf() for _ in range(NSLOT)])
        sl32 = ([C.dram_scr("cc_s32_%d" % i, [512, D], F32) for i in range(NSLOT)], [C.dram_scr("cc_r32_%d" % i, [1024, D], F32) for i in range(NSLOT)],
                [C.buf() for _ in range(NSLOT)], [C.buf() for _ in range(NSLOT)])

        on0_snd = C.dram_scr("on0_snd", [512, S], BF16)
        on0_all = C.dram_scr("on0_all", [1024, S], BF16)
        emit_A(C, dict(x=C.dram_in("xb", [S, D], F32), wq=C.dram_in("wq", [D, 512], F32), wk=C.dram_in("wk", [D, 512], F32),
                       wv=C.dram_in("wv", [D, 512], F32), wf=C.dram_in("wf", [D, 8], F32), bf=C.dram_in("bf", [128, 8], F32),
                       g=C.dram_in("gA", [128, 8], F32), out=on0_snd,
                       after_head=make_head_exchange(C, groups, on0_snd, lambda r, i: on0_all[r * 512 + i * 128:r * 512 + (i + 1) * 128, :], 2, sl16)))
        import os
        STOP = int(os.environ.get("FUSED_STOP", 99))
        if STOP >= 2:
            pass

        x_own = C.dram_in("x_own", [NTOK, D], F32)
        x1_snd = C.dram_scr("x1_snd", [OWN, D], F32)
        x1_all = C.dram_scr("x1_all", [S, D], F32)

        def mk_cands(all_ap):
            v = all_ap.rearrange("(c p) t -> p c t", p=128)

            def f(t0, T):
                if t0 == 0:
                    return [(v[:, :, OWN - HALO:OWN], 1)]
                r = t0 - HALO
                return [(v[:, :, r:r + T], 0), (v[:, :, OWN + r:OWN + r + T], 1)]
            return f

        def ffn_io(l, extra):
            d = dict(wo=C.dram_in("wo%d" % l, [(8 if l == 0 else 16) * 128, D], F32), g=C.dram_in("g%d" % l, [128, 8], F32),
                     w_in=C.dram_in("w_in%d" % l, [D, 2 * DFF], F32), cw=C.dram_in("cw%d" % l, [128, 2 * NFC, 3], F32),
                     cb=C.dram_in("cb%d" % l, [128, 2 * NFC], F32), w_out=C.dram_in("w_out%d" % l, [DFF, D], F32), rk=rk, rkB=rkB)
            d.update(extra)
            return d

        io0 = ffn_io(0, dict(
            out=x1_snd, on_cands=mk_cands(on0_all), x_src=lambda t0, T: (tok(x_own[t0:t0 + T, :]), None)))
        if STOP >= 3:
            emit_B(C, 8, False, "L0_", io0)
        if STOP >= 4:
            pass

        on1_snd = C.dram_scr("on1_snd", [1024, S], BF16)
        on1_all = C.dram_scr("on1_all", [2048, S], BF16)
        ioC = (dict(x=x1_all, wdkv=C.dram_in("wdkv", [D, 320], F32), gkv=C.dram_in("gkv", [128, 8], F32),
                       kvn=C.dram_in("kvn", [128, 2], F32), wuk=C.dram_in("wuk", [256, 1024], F32), wuv=C.dram_in("wuv", [256, 1024], F32),
                       gmla=C.dram_in("gmla", [128, 8], F32), wdq=C.dram_in("wdq", [D, 768], F32), qn=C.dram_in("qn", [128, 6], F32),
                       wuqn=C.dram_in("wuqn", [768, 1024], F32), wuqr=C.dram_in("wuqr", [768, 512], F32),
                       cos2=C.dram_in("cos2", [64, S], F32), sin2=C.dram_in("sin2", [64, S], F32), out=on1_snd,
                       after_head=make_head_exchange(C, groups, on1_snd, lambda r, i: on1_all[r * 1024 + i * 128:r * 1024 + (i + 1) * 128, :], 1, sl16),
                       pre_main=lambda: exchange(C, groups, x1_snd, lambda r, i: x1_all[r * OWN + i * 512:r * OWN + (i + 1) * 512, :], 8, 512, sl32,
                                                 standalone=False)))
        if STOP >= 5:
            emit_C(C, ioC)
        if STOP >= 6:
            pass

        out_d = C.dram_out("out", [OWN, D], F32)

        def x_src1(t0, T):
            if t0 == 0:
                return (tok(x1_all[OWN - HALO:OWN, :]), 1)
            r = t0 - HALO
            return (tok(x1_snd[r:r + T, :]), None)

        io1 = ffn_io(1, dict(
            out=out_d, gfin=C.dram_in("gfin", [128, D], F32), on_cands=mk_cands(on1_all), x_src=x_src1))
        if STOP >= 7:
            emit_B(C, 16, True, "L1_", io1)
    return nc


def inputs_fused(inp, b, hh, consts, tabs):
    x = np.asarray(inp["x"], np.float32)
    dA = inputs_A(inp, b, hh, consts)
    d = {"xb": dA["x"], "wq": dA["wq"], "wk": dA["wk"], "wv": dA["wv"], "wf": dA["wf"], "bf": dA["bf"], "gA": dA["gcol"]}
    xo = np.zeros((NTOK, D), np.float32)
    t0 = hh * OWN
    if t0 > 0:
        xo[0:HALO] = x[b, t0 - HALO:t0]
    xo[HALO:] = x[b, t0:t0 + OWN]
    d["x_own"] = xo
    for l in range(2):
        cw = np.asarray(inp["ffn_conv_w"][l], np.float32)
        d["wo%d" % l] = np.ascontiguousarray(np.asarray(inp["w_fox_out"][0] if l == 0 else inp["w_mla_out"][0], np.float32))
        d["g%d" % l] = col128(inp["ffn_norm"][l])
        d["w_in%d" % l] = np.ascontiguousarray(np.asarray(inp["w_ffn_in"][l], np.float32))
        d["cw%d" % l] = np.ascontiguousarray(cw.T.reshape(2 * NFC, 128, 3).transpose(1, 0, 2))
        d["cb%d" % l] = col128(inp["ffn_conv_b"][l])
        d["w_out%d" % l] = np.ascontiguousarray(np.asarray(inp["w_ffn_out"][l], np.float32))
    d["gfin"] = np.ascontiguousarray(np.broadcast_to(np.asarray(inp["final_norm"], np.float32)[None, :], (128, D)))
    dC = inputs_C(inp, x[b], hh, consts, tabs)
    for k_ in ("wdkv", "gkv", "kvn", "wuk", "wuv", "gmla", "wdq", "qn", "wuqn", "wuqr", "cos2", "sin2"):
        d[k_] = dC[k_]
    rk = np.zeros((128, 4), np.float32)
    rk[:, hh] = 1.0
    d["rk"] = rk
    d.update(consts)
    return d


def kernel(**inp):
    cst = host_consts()
    tabs = rope_tables()
    cores = list(range(8))
    nc = build_fused(8)
    maps = [inputs_fused(inp, c // 2, c % 2, cst, tabs) for c in cores]
    res = run_bass_kernel_spmd(nc, maps, core_ids=cores).results
    out = np.stack([np.concatenate([np.asarray(res[2 * b]["out"]), np.asarray(res[2 * b + 1]["out"])], 0) for b in range(NB)], 0)
    return out.astype(np.float32)
```
